# Optimizing a Trainium2 kernel written in Bass

```python
import math
import jax, jax.numpy as jnp
from jax import lax
import numpy as np

D_MODEL = 2048
BATCH = 4
SEQ = 4096
DEPTH = 2

N_A_LAYERS = DEPTH // 2
N_B_LAYERS = DEPTH - N_A_LAYERS
HEAD_DIM = 128
MEM_LEN = 256
MEM_HEADS = 4
MEM_WIDTH = MEM_HEADS * HEAD_DIM
MIX_WIDTH = D_MODEL
TOK_WIDTH = MIX_WIDTH - MEM_WIDTH
CHUNK = 128
SGU_GROUPS = TOK_WIDTH // HEAD_DIM
MLA_HEADS = TOK_WIDTH // HEAD_DIM
QK_NOPE = 128
QK_ROPE = 64
V_DIM = 128
Q_LORA = 512
KV_LORA = 512
ROPE_THETA = 10000.0
D_FF = ((8 * D_MODEL + 3 * 256 - 1) // (3 * 256)) * 256
Q_BLOCK = 128
EPS = 1e-6
MLA_SCALE = (QK_NOPE + QK_ROPE) ** -0.5
MEM_SCALE = HEAD_DIM ** -0.5
A_IN_WIDTH = 2 * TOK_WIDTH + MEM_WIDTH
B_IN_WIDTH = Q_LORA + MEM_WIDTH

kernel_name = "yoco_gmlp_mla_memory_hybrid"


def rms_norm(x, g):
    xf = x.astype(jnp.float32)
    y = xf * lax.rsqrt(jnp.mean(xf * xf, axis=-1, keepdims=True) + EPS)
    return (y * g.astype(jnp.float32)).astype(x.dtype)


def layer_norm(x, g, b):
    xf = x.astype(jnp.float32)
    mu = jnp.mean(xf, axis=-1, keepdims=True)
    xc = xf - mu
    y = xc * lax.rsqrt(jnp.mean(xc * xc, axis=-1, keepdims=True) + EPS)
    return (y * g.astype(jnp.float32) + b.astype(jnp.float32)).astype(x.dtype)


def rope_tables(positions):
    inv_freq = ROPE_THETA ** (-jnp.arange(0, QK_ROPE, 2, dtype=jnp.float32) / QK_ROPE)
    ang = positions.astype(jnp.float32)[..., None] * inv_freq
    return jnp.cos(ang), jnp.sin(ang)


def apply_rope(x, cos, sin):
    x1, x2 = jnp.split(x.astype(jnp.float32), 2, axis=-1)
    out = jnp.concatenate([x1 * cos - x2 * sin, x2 * cos + x1 * sin], axis=-1)
    return out.astype(x.dtype)


def swiglu_ffn(h, w_gate, w_up, w_down):
    return (jax.nn.silu(h @ w_gate) * (h @ w_up)) @ w_down


def memory_attention(q, mem, mem_g, w_mem_kv):
    B, S, _ = q.shape
    kv = rms_norm(mem, mem_g) @ w_mem_kv
    k, v = jnp.split(kv, 2, axis=-1)
    k = k.reshape(B, -1, MEM_HEADS, HEAD_DIM)
    v = v.reshape(B, -1, MEM_HEADS, HEAD_DIM)
    q = q.reshape(B, S, MEM_HEADS, HEAD_DIM)
    s = jnp.einsum('bqhd,bmhd->bhqm', q, k).astype(jnp.float32) * MEM_SCALE
    p = jax.nn.softmax(s, axis=-1).astype(v.dtype)
    return jnp.einsum('bhqm,bmhd->bqhd', p, v).reshape(B, S, MEM_WIDTH)


def gmlp_spatial_gating(z, ln_g, ln_b, w_s, b_s):
    B, S, _ = z.shape
    z = jax.nn.gelu(z)
    u, v = jnp.split(z, 2, axis=-1)
    v = layer_norm(v, ln_g, ln_b)
    v = v.reshape(B, S // CHUNK, CHUNK, SGU_GROUPS, HEAD_DIM)
    causal = jnp.tril(jnp.ones((CHUNK, CHUNK), dtype=bool))
    w = jnp.where(causal[None], w_s, jnp.zeros_like(w_s))
    s = jnp.einsum('gts,bnsgc->bntgc', w, v) + b_s.T[None, None, :, :, None]
    return u * s.reshape(B, S, TOK_WIDTH)


def causal_mla(q_nope, q_rope, k_nope, k_rope, v):
    B, S, H, _ = q_nope.shape
    nb = S // Q_BLOCK
    qn = q_nope.reshape(B, nb, Q_BLOCK, H, QK_NOPE).transpose(1, 0, 2, 3, 4)
    qr = q_rope.reshape(B, nb, Q_BLOCK, H, QK_ROPE).transpose(1, 0, 2, 3, 4)
    k_idx = jnp.arange(S)

    def block(args):
        i, qn_b, qr_b = args
        s = (jnp.einsum('bqhd,bkhd->bhqk', qn_b, k_nope)
             + jnp.einsum('bqhr,bkr->bhqk', qr_b, k_rope)).astype(jnp.float32) * MLA_SCALE
        q_idx = i * Q_BLOCK + jnp.arange(Q_BLOCK)
        mask = q_idx[:, None] >= k_idx[None, :]
        s = jnp.where(mask[None, None], s, jnp.finfo(jnp.float32).min)
        p = jax.nn.softmax(s, axis=-1).astype(v.dtype)
        return jnp.einsum('bhqk,bkhd->bqhd', p, v)

    out = lax.map(block, (jnp.arange(nb), qn, qr))
    return out.transpose(1, 0, 2, 3, 4).reshape(B, S, H * V_DIM)


def shared_latent_kv(h, kv_src_norm, w_kv_a, kv_norm, w_uk, w_uv, cos, sin):
    B, S, _ = h.shape
    a = rms_norm(h, kv_src_norm) @ w_kv_a
    c_kv = rms_norm(a[..., :KV_LORA], kv_norm)
    k_rope = apply_rope(a[..., KV_LORA:], cos, sin)
    k_nope = (c_kv @ w_uk).reshape(B, S, MLA_HEADS, QK_NOPE)
    v = (c_kv @ w_uv).reshape(B, S, MLA_HEADS, V_DIM)
    return k_nope, k_rope, v


def setup_inputs(seed: int = 0) -> dict:
    key = jax.random.key(seed)
    ks = iter(jax.random.split(key, 32))
    f32 = jnp.float32

    def w(shape, fan_in):
        return jax.random.normal(next(ks), shape, f32) * (fan_in ** -0.5)

    def gain(shape):
        return 1.0 + 0.02 * jax.random.normal(next(ks), shape, f32)

    x = jax.random.normal(next(ks), (BATCH, SEQ, D_MODEL), f32)
    mem = jax.random.normal(next(ks), (BATCH, MEM_LEN, D_MODEL), f32)
    offset = jax.random.randint(next(ks), (BATCH, 1), 0, 4096, dtype=jnp.int32)
    positions = offset + jnp.arange(SEQ, dtype=jnp.int32)[None, :]
    return {
        "x": x,
        "mem": mem,
        "positions": positions,
        "norm_gains": gain((DEPTH, 4, D_MODEL)),
        "mem_norm": gain((DEPTH, D_MODEL)),
        "w_mem_kv": w((DEPTH, D_MODEL, 2 * MEM_WIDTH), D_MODEL),
        "ffn_w_gate": w((DEPTH, D_MODEL, D_FF), D_MODEL),
        "ffn_w_up": w((DEPTH, D_MODEL, D_FF), D_MODEL),
        "ffn_w_down": w((DEPTH, D_FF, D_MODEL), D_FF),
        "a_w_in": w((N_A_LAYERS, D_MODEL, A_IN_WIDTH), D_MODEL),
        "a_ln_g": gain((N_A_LAYERS, TOK_WIDTH)),
        "a_ln_b": 0.02 * jax.random.normal(next(ks), (N_A_LAYERS, TOK_WIDTH), f32),
        "a_w_s": w((N_A_LAYERS, SGU_GROUPS, CHUNK, CHUNK), CHUNK),
        "a_b_s": gain((N_A_LAYERS, SGU_GROUPS, CHUNK)),
        "a_w_out": w((N_A_LAYERS, MIX_WIDTH, D_MODEL), MIX_WIDTH),
        "kv_src_norm": gain((D_MODEL,)),
        "w_kv_a": w((D_MODEL, KV_LORA + QK_ROPE), D_MODEL),
        "kv_norm": gain((KV_LORA,)),
        "w_uk": w((KV_LORA, MLA_HEADS * QK_NOPE), KV_LORA),
        "w_uv": w((KV_LORA, MLA_HEADS * V_DIM), KV_LORA),
        "b_w_in": w((N_B_LAYERS, D_MODEL, B_IN_WIDTH), D_MODEL),
        "b_q_norm": gain((N_B_LAYERS, Q_LORA)),
        "b_w_uq": w((N_B_LAYERS, Q_LORA, MLA_HEADS * (QK_NOPE + QK_ROPE)), Q_LORA),
        "b_w_out": w((N_B_LAYERS, MIX_WIDTH, D_MODEL), MIX_WIDTH),
    }


def reference(x, mem, positions, norm_gains, mem_norm, w_mem_kv, ffn_w_gate, ffn_w_up,
              ffn_w_down, a_w_in, a_ln_g, a_ln_b, a_w_s, a_b_s, a_w_out, kv_src_norm,
              w_kv_a, kv_norm, w_uk, w_uv, b_w_in, b_q_norm, b_w_uq, b_w_out):
    B, S, _ = x.shape
    cos, sin = rope_tables(positions)
    h = x
    k_nope = k_rope = v_shared = None
    for i in range(DEPTH):
        hn = rms_norm(h, norm_gains[i, 0])
        if i < N_A_LAYERS:
            j = i
            z = hn @ a_w_in[j]
            tok = gmlp_spatial_gating(z[..., :2 * TOK_WIDTH], a_ln_g[j], a_ln_b[j],
                                      a_w_s[j], a_b_s[j])
            memo = memory_attention(z[..., 2 * TOK_WIDTH:], mem, mem_norm[i], w_mem_kv[i])
            mix = jnp.concatenate([tok, memo], axis=-1) @ a_w_out[j]
        else:
            j = i - N_A_LAYERS
            if j == 0:
                k_nope, k_rope, v_shared = shared_latent_kv(
                    h, kv_src_norm, w_kv_a, kv_norm, w_uk, w_uv, cos, sin)
            z = hn @ b_w_in[j]
            cq = rms_norm(z[..., :Q_LORA], b_q_norm[j])
            q = (cq @ b_w_uq[j]).reshape(B, S, MLA_HEADS, QK_NOPE + QK_ROPE)
            q_nope = q[..., :QK_NOPE]
            q_rope = apply_rope(q[..., QK_NOPE:], cos[:, :, None, :], sin[:, :, None, :])
            att = causal_mla(q_nope, q_rope, k_nope, k_rope, v_shared)
            memo = memory_attention(z[..., Q_LORA:], mem, mem_norm[i], w_mem_kv[i])
            mix = jnp.concatenate([att, memo], axis=-1) @ b_w_out[j]
        h = h + rms_norm(mix, norm_gains[i, 1])
        f = swiglu_ffn(rms_norm(h, norm_gains[i, 2]), ffn_w_gate[i], ffn_w_up[i], ffn_w_down[i])
        h = h + rms_norm(f, norm_gains[i, 3])
    return h
```

```python
import math
from contextlib import ExitStack
import numpy as np
import ml_dtypes
import concourse.bass as bass
import concourse.mybir as mybir
from concourse.bass_utils import run_bass_kernel_spmd

F32 = mybir.dt.float32
BF16 = mybir.dt.bfloat16
I32 = mybir.dt.int32
AF = mybir.ActivationFunctionType
ALU = mybir.AluOpType

D = 2048
NKC = 16
TT = 512
NTILE = 4
TOKC = 2048
SEQ = 4096
DFF = 5632
NFC = 44
EPS = 1e-6
MLA_SCALE = 192 ** -0.5
MEM_SCALE = 128 ** -0.5
TWO_PI = 2.0 * math.pi


class Buf:
    __slots__ = ("w", "r")

    def __init__(self):
        self.w = None
        self.r = {}


class Tl:
    __slots__ = ("ap", "bufs")

    def __init__(self, ap, bufs):
        self.ap = ap
        self.bufs = bufs

    def __getitem__(self, key):
        return Tl(self.ap[key], self.bufs)


class Prog:
    ENGS = ("pe", "act", "dve", "pool", "sp")

    def __init__(self, nc, es, n_sp_sems=8):
        self.nc = nc
        self.ins = {e: [] for e in self.ENGS}
        self.cnt = {e: 0 for e in self.ENGS}
        self.known = {e: {} for e in self.ENGS}
        self.esem = {e: es.enter_context(nc.semaphore("sem_" + e)) for e in self.ENGS}
        self.sp_sems = [es.enter_context(nc.semaphore("spd%d" % i)) for i in range(n_sp_sems)]
        self.sp_cnt = [0] * n_sp_sems
        self.sp_next = 0
        self.n_ins = 0
        act_sems = [es.enter_context(nc.semaphore("actd%d" % i)) for i in range(4)]
        self.rings = {"sp": [self.sp_sems, self.sp_cnt, 0], "act": [act_sems, [0] * 4, 0]}

    def custom(self, eng, fn, reads, writes, sem, amount):
        waits = self._deps(eng, reads, writes)
        self.ins[eng].append((waits, fn, (sem, amount)))
        self._mark((sem, amount), reads, writes)
        self.n_ins += 1

    def _deps(self, eng, reads, writes):
        waits = {}
        known = self.known[eng]
        pe_sem = self.esem["pe"]

        def need(sem, val):
            if eng == "pe" and sem is pe_sem:
                return
            if known.get(sem, 0) >= val:
                return
            if waits.get(sem, 0) < val:
                waits[sem] = val

        for t in reads:
            for b in t.bufs:
                if b.w is not None:
                    need(*b.w)
        for t in writes:
            for b in t.bufs:
                if b.w is not None:
                    need(*b.w)
                for s, v in b.r.items():
                    need(s, v)
        for s, v in waits.items():
            known[s] = v
        return list(waits.items())

    def _mark(self, tok, reads, writes):
        s, v = tok
        for t in reads:
            for b in t.bufs:
                if b.r.get(s, 0) < v:
                    b.r[s] = v
        for t in writes:
            for b in t.bufs:
                b.w = tok
                b.r = {}

    def op(self, eng, fn, reads=(), writes=(), signal=True):
        waits = self._deps(eng, reads, writes)
        if signal:
            self.cnt[eng] += 1
            tok = (self.esem[eng], self.cnt[eng])
        else:
            tok = (self.esem[eng], self.cnt[eng] + 1)
        self.ins[eng].append((waits, fn, (self.esem[eng], 1) if signal else None))
        self._mark(tok, reads, writes)
        self.n_ins += 1
        return tok

    def dma(self, eng, out_ap, in_ap, reads=(), writes=(), sem=None, semcnt=None, chain=False):
        if sem is None:
            rg = self.rings[eng]
            j = rg[2]
            rg[2] = (j + 1) % len(rg[0])
            sem = rg[0][j]
            prev = rg[1][j]
            rg[1][j] += 16
            val = rg[1][j]
        else:
            prev = semcnt[0]
            semcnt[0] += 16
            val = semcnt[0]
        waits = dict(self._deps(eng, reads, writes))
        if (not chain) and prev > 0 and self.known[eng].get(sem, 0) < prev:
            waits[sem] = max(waits.get(sem, 0), prev)
            self.known[eng][sem] = prev
        self.ins[eng].append((list(waits.items()),
                              lambda e, o=out_ap, i=in_ap: e.dma_start(out=o, in_=i), (sem, 16)))
        tok = (sem, val)
        self._mark(tok, reads, writes)
        self.n_ins += 1
        return tok

    def emit(self, final_waits):
        nc = self.nc
        prog = self

        def run(eng_name, e):
            for waits, fn, inc in prog.ins[eng_name]:
                for s, v in waits:
                    e.wait_ge(s, v)
                ins = fn(e)
                if inc is not None:
                    ins.then_inc(inc[0], inc[1])
            for s, v in final_waits.get(eng_name, []):
                e.wait_ge(s, v)

        with nc.Block() as block:
            @block.tensor
            def _(e):
                run("pe", e)

            @block.scalar
            def _(e):
                run("act", e)

            @block.vector
            def _(e):
                run("dve", e)

            @block.gpsimd
            def _(e):
                run("pool", e)

            @block.sync
            def _(e):
                run("sp", e)


class Ctx:
    pass


class StopBuild(Exception):
    pass


def build(mode, stop=None):
    nc = bass.Bass("TRN2", target_bir_lowering=False)
    es = ExitStack()
    P = Prog(nc, es)
    c = Ctx()

    def stage(n):
        if stop is not None and n > stop:
            raise StopBuild()

    doA = mode in ("A", "AB")
    doB = mode in ("B", "AB")

    def din(name, shape, dt=F32):
        return nc.dram_tensor(name, list(shape), dt, kind="ExternalInput").ap()

    def dout(name, shape, dt=F32):
        return nc.dram_tensor(name, list(shape), dt, kind="ExternalOutput").ap()

    def dint(name, shape, dt=F32):
        return nc.dram_tensor(name, list(shape), dt, kind="Internal").ap()

    vecs = din("vecs", [256, 128])
    cst = din("cst", [128, 512])
    invf = din("invf", [64, 1])
    pos = din("pos", [1, TOKC], I32)
    mem = din("mem", [256, D])
    layers = ([0] if doA else []) + ([1] if doB else [])
    w_mem_kv = {l: din("w_mem_kv%d" % l, [D, 1024]) for l in layers}
    ffn_w_gate = {l: din("ffn_w_gate%d" % l, [D, DFF]) for l in layers}
    ffn_w_up = {l: din("ffn_w_up%d" % l, [D, DFF]) for l in layers}
    ffn_w_down = {l: din("ffn_w_down%d" % l, [DFF, D]) for l in layers}
    if doA:
        x = din("x", [TOKC, D])
        a_w_in = din("a_w_in", [1, D, 3584])
        a_w_s = din("a_w_s", [1, 12, 128, 128])
        a_b_s = din("a_b_s", [1, 12, 128])
        a_w_out = din("a_w_out", [1, D, D])
        w_kv_a = din("w_kv_a", [D, 576])
        w_uk = din("w_uk", [512, 1536])
        w_uv = din("w_uv", [512, 1536])
    if doB:
        b_w_in = din("b_w_in", [1, D, 1024])
        b_w_uq = din("b_w_uq", [1, 512, 2304])
        b_w_out = din("b_w_out", [1, D, D])
        out = dout("out", [TOKC, D])
    if mode == "A":
        hT_d = dout("hT", [D, TOKC])
        KT_loc = dout("KT", [12, 128, TOKC], BF16)
        krT_loc = dout("krT", [64, TOKC], BF16)
        V_loc = dout("V", [TOKC, 1536], BF16)
    elif mode == "B":
        hT_d = din("hT", [D, TOKC])
        KT_full = din("KT", [12, 128, SEQ], BF16)
        krT_full = din("krT", [64, SEQ], BF16)
        V_full = din("V", [SEQ, 1536], BF16)
    else:
        hT_d = dint("hT_scr", [D, TOKC])
        l1 = [dint("l1_%d" % t, [1600, TT], BF16) for t in range(NTILE)]
        l2 = [dint("l2_%d" % t, [1536, TT], BF16) for t in range(NTILE)]
        g1 = [dint("g1_%d" % t, [2 * 1600, TT], BF16) for t in range(NTILE)]
        g2 = [dint("g2_%d" % t, [2 * 1536, TT], BF16) for t in range(NTILE)]
        l2v = [a_.rearrange("r c -> (r c)").rearrange("(t d) -> t d", d=1536) for a_ in l2]
        g2v = [[g2[t][rk * 1536:(rk + 1) * 1536, :].rearrange("r c -> (r c)").rearrange("(t d) -> t d", d=1536)
                for rk in range(2)] for t in range(NTILE)]
        ccsems = [es.enter_context(nc.semaphore("ccs%d" % i)) for i in range(2 * NTILE)]
        g1dep = [Tl(None, [Buf()]) for _ in range(NTILE)]
        g2dep = [Tl(None, [Buf()]) for _ in range(NTILE)]
    if mode == "B":
        KTg = [KT_full[:, :, rk * TOKC:(rk + 1) * TOKC] for rk in range(2)]
        krTg = [krT_full[:, rk * TOKC:(rk + 1) * TOKC] for rk in range(2)]
        Vg = [V_full[rk * TOKC:(rk + 1) * TOKC, :] for rk in range(2)]
    kvst_bufs = []
    kvg_dep = Tl(None, [Buf()])
    hdep = [[Tl(None, [Buf()]) for _ in range(NKC)] for _ in range(NTILE)]

    def sb(name, shape, dt):
        return es.enter_context(nc.sbuf_tensor("sb_" + name, list(shape), dt))

    def gran(n):
        return [Buf() for _ in range(n)]

    hA = sb("hA", [128, NKC, TT], F32)
    h = [Tl(hA[:, k, :], [Buf()]) for k in range(NKC)]

    regB = sb("regB", [128, 8192], F32)
    gB = gran(32)
    regC = sb("regC", [128, 11264], F32)
    gC = gran(45)

    def view(reg, grans, f0, f1, dt, inner):
        ap = reg[:, f0:f1]
        if dt == BF16:
            ap = ap.bitcast(BF16)
            nel = (f1 - f0) * 2
            bpe = 2
        else:
            nel = f1 - f0
            bpe = 4
        n = nel // inner
        ap3 = ap.rearrange("p (k n) -> p k n", n=inner)
        tiles = []
        for k in range(n):
            b0 = (f0 * 4 + k * inner * bpe) // 1024
            b1 = (f0 * 4 + (k + 1) * inner * bpe - 1) // 1024
            tiles.append(Tl(ap3[:, k, :], grans[b0:b1 + 1]))
        return tiles

    hn = view(regB, gB, 0, 4096, BF16, TT)
    tokT = view(regB, gB, 4096, 7168, BF16, TT)
    memoT = view(regB, gB, 7168, 8192, BF16, TT)
    fT = view(regB, gB, 0, 8192, F32, TT)
    vf = view(regC, gC, 0, 6144, F32, 1536)
    vn = view(regC, gC, 6144, 9216, BF16, 1536)
    qmT = view(regC, gC, 9216, 10240, BF16, TT)
    mixT = view(regC, gC, 0, 8192, F32, TT)
    actT = view(regC, gC, 0, 11264, BF16, TT)
    xs = view(regC, gC, 0, 8192, F32, D)
    cqT = view(regC, gC, 8192, 9216, BF16, TT)
    qnT = view(regC, gC, 0, 3072, BF16, TT)
    qrT = view(regC, gC, 3072, 6144, BF16, TT)
    cqf = view(regC, gC, 6144, 8192, F32, TT)
    ckf = view(regC, gC, 0, 2048, F32, TT)
    ckT = view(regC, gC, 2048, 3072, BF16, TT)
    kst = view(regC, gC, 3072, 4096, BF16, TT)
    vst = view(regC, gC, 4096, 5632, BF16, 1536)
    krs = view(regC, gC, 5632, 6144, BF16, TT)
    memst = view(regC, gC, 0, 4096, F32, D)
    memTf = view(regC, gC, 4096, 6144, F32, 256)
    memTf2 = view(regC, gC, 6144, 8192, F32, 256)
    memTf = memTf + memTf2
    memnT = view(regC, gC, 8192, 10240, BF16, 256)

    NS = 3
    wslot_t = sb("wslots", [128, NS, 8192], BF16)
    wslots = [Tl(wslot_t[:, s, :], [Buf()]) for s in range(NS)]
    wsem = [es.enter_context(nc.semaphore("wsem%d" % s)) for s in range(NS)]
    wcnt = [[0] for _ in range(NS)]
    c.wnext = 0
    c.cc_pending = []
    cc_sem = es.enter_context(nc.semaphore("cc_sem"))

    cstf_t = sb("cstf", [128, 256], F32)
    cstf = Tl(cstf_t[:, :], [Buf()])
    ident = cstf[:, 0:128]
    triu = cstf[:, 128:256]
    dmask_t = sb("dmask", [128, 256], BF16)
    dmask = Tl(dmask_t[:, :], [Buf()])
    ones_t = sb("ones", [128, 128], BF16)
    ones = Tl(ones_t[:, :], [Buf()])
    gv_t = sb("gv", [128, 256], F32)
    gv = Tl(gv_t[:, :], [Buf()])
    invf_t = sb("invf", [64, 1], F32)
    invfT = Tl(invf_t[:, :], [Buf()])
    sq_t = sb("sq", [128, 4, TT], BF16)
    sq = [Tl(sq_t[:, i, :], [Buf()]) for i in range(4)]
    c.sqn = 0
    tmp_t = sb("tmpf", [128, 4, TT], F32)
    tmpf = [Tl(tmp_t[:, i, :], [Buf()]) for i in range(4)]
    c.tmpn = 0
    rstd_t = sb("rstd", [128, 2, TT], F32)
    rstd = [Tl(rstd_t[:, i, :], [Buf()]) for i in range(2)]
    c.rstdn = 0
    pT_t = sb("pT", [128, 4, TT], BF16)
    pT = [Tl(pT_t[:, i, :], [Buf()]) for i in range(4)]
    c.pTn = 0
    KmT_t = sb("KmT", [128, 4, 256], BF16)
    KmT = Tl(KmT_t[:, :, :], [Buf()])
    Vm_t = sb("Vm", [128, 2, 512], BF16)
    Vm = Tl(Vm_t[:, :, :], [Buf()])
    tab_t = sb("tabs", [64, 2, TT], F32)
    Ctab = Tl(tab_t[:, 0, :], [Buf()])
    Stab = Tl(tab_t[:, 1, :], [Buf()])
    st_t = sb("stats", [128, 4, 32], F32)
    stt = [Tl(st_t[:, i, :], [Buf()]) for i in range(4)]
    regD = sb("regD", [128, 5120], F32)
    gD = gran(20)
    if doA:
        wsT = Tl(regD[:, 0:768].bitcast(BF16).rearrange("p (g t) -> p g t", t=128), gD[0:3])
        Bt = Tl(regD[:, 768:2304].rearrange("p (g t) -> p g t", t=128), gD[3:9])
        bsb = view(regC, gC, 8192, 9728, F32, 1536)[0]
    if doB:
        krTs = Tl(regD[:, 0:2048].bitcast(BF16), gD[0:8])
        NKV = 6
        Kc = [Tl(regD[:, 2048 + i * 256:2048 + (i + 1) * 256].bitcast(BF16), gD[8 + i:9 + i]) for i in range(NKV)]
        Vc = [Tl(regD[:, 3584 + i * 256:3584 + (i + 1) * 256].bitcast(BF16).rearrange("p (b d) -> p b d", d=128),
                 gD[14 + i:15 + i]) for i in range(NKV)]
        c.kvn = 0

    banks = []
    for i in range(8):
        t = es.enter_context(nc.psum_tensor("ps%d" % i, [128, 512], F32))
        banks.append(Tl(t[:, :], [Buf()]))
    c.bank = 0
    c.ring = [0, 1, 2, 3, 4, 5, 6]
    STAT = banks[7]

    def nb():
        c.bank = (c.bank + 1) % len(c.ring)
        return banks[c.ring[c.bank]]

    def ring(lst, attr):
        i = getattr(c, attr)
        setattr(c, attr, (i + 1) % len(lst))
        return lst[i]

    def mm(out, lhsT, rhs, start, stop, signal=None):
        if signal is None:
            signal = stop
        P.op("pe", lambda e, o=out.ap, l=lhsT.ap, r=rhs.ap, s=start, t=stop: e.matmul(o, l, r, start=s, stop=t),
             reads=[lhsT, rhs], writes=[out], signal=signal)

    def tr(out, in_, signal=True):
        P.op("pe", lambda e, o=out.ap, i=in_.ap, d=ident.ap: e.transpose(o, i, d),
             reads=[in_, ident], writes=[out], signal=signal)

    def act(out, in_, func, scale=1.0, reads=(), bias=None):
        if bias is None:
            P.op("act", lambda e, o=out.ap, i=in_.ap, f=func, s=scale: e.activation(out=o, in_=i, func=f, scale=s),
                 reads=[in_] + list(reads), writes=[out])
        else:
            P.op("act", lambda e, o=out.ap, i=in_.ap, f=func, s=scale, b=bias: e.activation(out=o, in_=i, func=f, scale=s, bias=b),
                 reads=[in_] + list(reads), writes=[out])

    def vts(out, in0, s1, s2, op0, op1, reads=()):
        if s2 is None:
            P.op("dve", lambda e, o=out.ap, i=in0.ap: e.tensor_scalar(o, i, s1, None, op0),
                 reads=[in0] + list(reads), writes=[out])
        else:
            P.op("dve", lambda e, o=out.ap, i=in0.ap: e.tensor_scalar(o, i, s1, s2, op0, op1),
                 reads=[in0] + list(reads), writes=[out])

    def vstt(out, in0, scalar, in1, op0, op1, reads=()):
        P.op("dve", lambda e, o=out.ap, i0=in0.ap, i1=in1.ap: e.scalar_tensor_tensor(o, i0, scalar, i1, op0, op1),
             reads=[in0, in1] + list(reads), writes=[out])

    def vtt(out, in0, in1, op):
        P.op("dve", lambda e, o=out.ap, i0=in0.ap, i1=in1.ap: e.tensor_tensor(o, i0, i1, op),
             reads=[in0, in1], writes=[out])

    def vcopy(out, in_):
        P.op("dve", lambda e, o=out.ap, i=in_.ap: e.tensor_copy(o, i), reads=[in_], writes=[out])

    def vrecip(out, in_):
        P.op("dve", lambda e, o=out.ap, i=in_.ap: e.reciprocal(o, i), reads=[in_], writes=[out])

    c.cp = 0

    def evac_copy(out, in_):
        c.cp ^= 1
        if c.cp:
            act(out, in_, AF.Copy)
        else:
            vcopy(out, in_)

    def wload(parts):
        s = c.wnext
        c.wnext = (s + 1) % NS
        slot = wslots[s]
        for pi, (dst_ap, src_ap) in enumerate([(p[0](wslot_t, s), p[1]) for p in parts]):
            P.dma("pool", dst_ap, src_ap, reads=(), writes=[slot] if pi == 0 else [], sem=wsem[s], semcnt=wcnt[s],
                  chain=(pi > 0))
        slot.bufs[0].w = (wsem[s], wcnt[s][0])
        return s, slot

    def wview(s, nk, ncol):
        return wslot_t[:, s, 0:nk * ncol].rearrange("p (k n) -> p k n", n=ncol)

    def wload_std(wmat, k0, nk, c0, ncol):
        src = wmat[k0 * 128:(k0 + nk) * 128, c0:c0 + ncol].rearrange("(k p) n -> p k n", p=128)
        s, slot = wload([(lambda t, s_, nk=nk, ncol=ncol: t[:, s_, 0:nk * ncol].rearrange("p (k n) -> p k n", n=ncol), src)])
        return Tl(wview(s, nk, ncol), slot.bufs)

    gcol = lambda j: gv_t[:, j:j + 1]

    def stat_accum(stat_bank, src, first, last):
        s = ring(sq, "sqn")
        act(s, src, AF.Square)
        mm(stat_bank, ones, s, start=first, stop=last, signal=True)

    def make_rstd(stat_bank, dim):
        r = ring(rstd, "rstdn")
        act(r, stat_bank, AF.Sqrt, scale=1.0 / dim, bias=float(EPS))
        vrecip(r, r)
        return r

    def mm4_kmajor(w, rhs):
        bks = [nb() for _ in range(4)]
        nk = len(rhs)
        for k in range(nk):
            for mi in range(4):
                mm(bks[mi], w[:, k, mi * 128:(mi + 1) * 128], rhs[k], start=(k == 0), stop=(k == nk - 1),
                   signal=(k == nk - 1))
        return bks

    def rms_to(srcs, dsts, gj, dim, split=False):
        sbk = STAT
        n = len(srcs)
        for k in range(n):
            if split and (k % 2 == 1):
                s_ = ring(sq, "sqn")
                vtt(s_, srcs[k], srcs[k], ALU.mult)
                mm(sbk, ones, s_, start=(k == 0), stop=(k == n - 1), signal=True)
            else:
                stat_accum(sbk, srcs[k], k == 0, k == n - 1)
        r = make_rstd(sbk, dim)
        for k in range(n):
            vstt(dsts[k], srcs[k], gcol(gj + k), r, ALU.mult, ALU.mult, reads=[gv])

    def residual_add(srcs, r, gj):
        for k in range(NKC):
            vstt(srcs[k], srcs[k], gcol(gj + k), r, ALU.mult, ALU.mult, reads=[gv])
            vtt(h[k], h[k], srcs[k], ALU.add)

    try:
        P.dma("sp", cstf.ap, cst[:, 0:256], writes=[cstf])
        P.dma("pool", dmask.ap, cst[:, 256:512], writes=[dmask], sem=wsem[0], semcnt=wcnt[0])
        P.dma("sp", invfT.ap, invf[:, :], writes=[invfT])
        P.op("dve", lambda e: e.memset(ones_t[:, :], 1.0), writes=[ones])
        for half in range(2):
            st = xs[half][:, 0:128]
            P.dma("sp", st.ap, vecs[half * 128:(half + 1) * 128, :], writes=[st])
            bk = nb()
            tr(bk[:, 0:128], st)
            vcopy(gv[:, half * 128:(half + 1) * 128], bk[:, 0:128])
        stage(1)
        G_NORM = lambda l, n: l * 64 + n * 16
        G_KVSRC = 128
        G_MEM = lambda l: 144 + l * 16
        G_BQ = 176
        G_KVN = 180
        G_LNB = 184
        G_LNG = 196

        def setup_mem(l):
            for t in range(2):
                P.dma("sp", memst[t].ap, mem[t * 128:(t + 1) * 128, :], writes=[memst[t]])
            for k in range(NKC):
                bk = nb()
                for t in range(2):
                    tr(bk[:, t * 128:(t + 1) * 128], memst[t][:, k * 128:(k + 1) * 128], signal=(t == 1))
                evac_copy(memTf[k], bk[:, 0:256])
            sbk = STAT
            for k in range(NKC):
                s = ring(sq, "sqn")
                act(s[:, 0:256], memTf[k], AF.Square)
                mm(sbk[:, 0:256], ones, s[:, 0:256], start=(k == 0), stop=(k == NKC - 1), signal=True)
            r = ring(rstd, "rstdn")
            act(r[:, 0:256], sbk[:, 0:256], AF.Sqrt, scale=1.0 / D, bias=float(EPS))
            vrecip(r[:, 0:256], r[:, 0:256])
            for k in range(NKC):
                vstt(memnT[k], memTf[k], gcol(G_MEM(l) + k), r[:, 0:256], ALU.mult, ALU.mult, reads=[gv])
            wk = wload_std(w_mem_kv[l], 0, 16, 0, 512)
            for hh in range(4):
                bk = nb()
                for k in range(NKC):
                    mm(bk[:, 0:256], wk[:, k, hh * 128:(hh + 1) * 128], memnT[k], start=(k == 0), stop=(k == NKC - 1))
                evac_copy(KmT[:, hh, :], bk[:, 0:256])
            wv = wload_std(w_mem_kv[l], 0, 16, 512, 512)
            for mt in range(2):
                bk = nb()
                for k in range(NKC):
                    mm(bk, memnT[k][:, mt * 128:(mt + 1) * 128], wv[:, k, :], start=(k == 0), stop=(k == NKC - 1))
                evac_copy(Vm[:, mt, :], bk)

        def mem_attention():
            def pv(hh, pts):
                ob = nb()
                sbk = nb()
                for mt in range(2):
                    mm(ob, Vm[:, mt, hh * 128:(hh + 1) * 128], pts[mt], start=(mt == 0), stop=(mt == 1))
                for mt in range(2):
                    mm(sbk, ones, pts[mt], start=(mt == 0), stop=(mt == 1))
                rc = ring(tmpf, "tmpn")
                vrecip(rc, sbk)
                vtt(memoT[hh], ob, rc, ALU.mult)

            prev = None
            for hh in range(4):
                pts = []
                for mt in range(2):
                    bk = nb()
                    mm(bk, KmT[:, hh, mt * 128:(mt + 1) * 128], qmT[hh], start=True, stop=True)
                    p = ring(pT, "pTn")
                    act(p, bk, AF.Exp, scale=MEM_SCALE)
                    pts.append(p)
                if prev is not None:
                    pv(*prev)
                prev = (hh, pts)
            pv(*prev)

        def tables(t0):
            pt_ = ring(tmpf, "tmpn")
            posi = Tl(pt_.ap[0:64, :].bitcast(I32), pt_.bufs)
            P.dma("sp", posi.ap, pos[0, t0:t0 + TT].partition_broadcast(64), writes=[posi])
            a0 = ring(tmpf, "tmpn")[0:64, :]
            a1 = ring(tmpf, "tmpn")[0:64, :]
            C1 = 6.28125
            C2 = TWO_PI - 6.28125
            vcopy(a0, posi)
            vts(a0, a0, invf_t[:, 0:1], None, ALU.mult, None, reads=[invfT])
            vts(a1, a0, float(1.0 / TWO_PI), None, ALU.mult, None)
            ki = Tl(posi.ap, posi.bufs)
            vcopy(ki, a1)
            vcopy(a1, ki)
            vstt(a0, a1, float(-C1), a0, ALU.mult, ALU.add)
            vstt(a0, a1, float(-C2), a0, ALU.mult, ALU.add)
            vts(a1, a0, float(math.pi / 2), float(-TWO_PI), ALU.is_gt, ALU.mult)
            vstt(a1, a0, float(math.pi / 2), a1, ALU.add, ALU.add)
            PI_ = 3.1415925
            vts(a1, a1, PI_, -PI_, ALU.min, ALU.max)
            vts(a0, a0, PI_, -PI_, ALU.min, ALU.max)
            act(Ctab, a1, AF.Sin, scale=1.0)
            act(Stab[0:32, :], a0[0:32, :], AF.Sin, scale=-1.0)
            act(Stab[32:64, :], a0[32:64, :], AF.Sin, scale=1.0)

        def rope_combine(dst, bk_a, bk_b):
            t1 = ring(tmpf, "tmpn")
            t2 = ring(tmpf, "tmpn")
            vtt(t1[0:64, :], bk_a[0:64, :], Ctab, ALU.mult)
            vtt(t2[0:64, :], bk_b[0:64, :], Stab, ALU.mult)
            vtt(dst[0:64, :], t1[0:64, :], t2[0:64, :], ALU.add)

        def out_proj_and_residual(wmat, rhs_list, gj):
            sbk = STAT
            pend_ = None
            for blk in range(4):
                w = wload_std(wmat, 0, 16, blk * 512, 512)
                for mi in range(4):
                    m = blk * 4 + mi
                    bk = nb()
                    for k in range(NKC):
                        mm(bk, w[:, k, mi * 128:(mi + 1) * 128], rhs_list[k], start=(k == 0), stop=(k == NKC - 1))
                    act(mixT[m], bk, AF.Copy)
                    s_ = ring(sq, "sqn")
                    act(s_, mixT[m], AF.Square)
                    if pend_ is not None:
                        mm(sbk, ones, pend_[1], start=(pend_[0] == 0), stop=False, signal=True)
                    pend_ = (m, s_)
            mm(sbk, ones, pend_[1], start=False, stop=True, signal=True)
            r = make_rstd(sbk, D)
            residual_add(mixT, r, gj)

        def ffn(l):
            rms_to(h, hn, G_NORM(l, 2), D)
            wg_m, wu_m, wd_m = ffn_w_gate[l], ffn_w_up[l], ffn_w_down[l]
            for blk in range(11):
                wg = wload_std(wg_m, 0, 16, blk * 512, 512)
                wu = wload_std(wu_m, 0, 16, blk * 512, 512)
                sgs = []
                bgs_ = mm4_kmajor(wg, hn) if blk == 0 else None
                for mi in range(4):
                    if bgs_ is not None:
                        bg = bgs_[mi]
                    else:
                        bg = nb()
                        for k in range(NKC):
                            mm(bg, wg[:, k, mi * 128:(mi + 1) * 128], hn[k], start=(k == 0), stop=(k == NKC - 1))
                    sg = ring(tmpf, "tmpn")
                    act(sg, bg, AF.Silu)
                    sgs.append(sg)
                for mi in range(4):
                    m = blk * 4 + mi
                    bu = nb()
                    for k in range(NKC):
                        mm(bu, wu[:, k, mi * 128:(mi + 1) * 128], hn[k], start=(k == 0), stop=(k == NKC - 1))
                    vtt(actT[m], sgs[mi], bu, ALU.mult)
            sbk = STAT
            dpend = []
            for blk in range(4):
                bks = [nb() for _ in range(4)]
                for kg in range(3):
                    nk = 16 if kg < 2 else 12
                    w = wload_std(wd_m, kg * 16, nk, blk * 512, 512)
                    for kl in range(nk):
                        k = kg * 16 + kl
                        for mi in range(4):
                            last_of_slot = (kl == nk - 1 and mi == 3)
                            stop = (k == NFC - 1)
                            mm(bks[mi], w[:, kl, mi * 128:(mi + 1) * 128], actT[k], start=(k == 0), stop=stop,
                               signal=(stop or last_of_slot))
                    if kg == 0:
                        for (m_, s_) in dpend:
                            mm(sbk, ones, s_, start=(m_ == 0), stop=False, signal=True)
                        dpend = []
                for mi in range(4):
                    m = blk * 4 + mi
                    act(fT[m], bks[mi], AF.Copy)
                    s_ = ring(sq, "sqn")
                    act(s_, fT[m], AF.Square)
                    dpend.append((m, s_))
            for (m_, s_) in dpend:
                mm(sbk, ones, s_, start=False, stop=(m_ == NKC - 1), signal=True)
            r = make_rstd(sbk, D)
            residual_add(fT, r, G_NORM(l, 3))

        if doA:
            for g in range(12):
                st = xs[0][:, g * 128:(g + 1) * 128]
                P.dma("sp", st.ap, a_w_s[0, g, :, :], writes=[st])
            P.dma("sp", bsb.ap, a_b_s[0, :, :].rearrange("g t -> (g t)").partition_broadcast(128), writes=[bsb])
            for g in range(12):
                bk = nb()
                tr(bk[:, 0:128], xs[0][:, g * 128:(g + 1) * 128])
                vtt(wsT[:, g, :], bk[:, 0:128], triu, ALU.mult)
            for g in range(12):
                bk = nb()
                mm(bk[:, 0:128], ones, wsT[:, g, :], start=True, stop=True)
                vstt(Bt[:, g, :], bk[:, 0:128], gcol(G_LNB + g), bsb[:, g * 128:(g + 1) * 128], ALU.mult, ALU.add, reads=[gv])
            stage(2)
            setup_mem(0)
            stage(3)

            for tt in range(NTILE):
                t0 = tt * TT
                for sub in range(4):
                    P.dma("act", xs[sub].ap, x[t0 + sub * 128:t0 + (sub + 1) * 128, :], writes=[xs[sub]])
                for sub in range(4):
                    for kq in range(4):
                        bk = nb()
                        for kk in range(4):
                            k = kq * 4 + kk
                            tr(bk[:, kk * 128:(kk + 1) * 128], xs[sub][:, k * 128:(k + 1) * 128], signal=(kk == 3))
                        o_tl = Tl(hA[:, kq * 4:(kq + 1) * 4, sub * 128:(sub + 1) * 128],
                                  [h[kq * 4 + kk].bufs[0] for kk in range(4)])
                        i_tl = Tl(bk.ap.rearrange("p (a b) -> p a b", b=128), bk.bufs)
                        evac_copy(o_tl, i_tl)
                stage(4 if tt == 0 else 13)
                rms_to(h, hn, G_NORM(0, 0), D, split=True)
                tables(t0)
                stage(5 if tt == 0 else 13)
                win = a_w_in[0]
                for blk in range(3):
                    w = wload_std(win, 0, 16, blk * 512, 512)
                    if blk == 0:
                        bks_ = mm4_kmajor(w, hn)
                        for mi in range(4):
                            act(tokT[mi], bks_[mi], AF.Gelu_apprx_tanh)
                        continue
                    for mi in range(4):
                        bk = nb()
                        for k in range(NKC):
                            mm(bk, w[:, k, mi * 128:(mi + 1) * 128], hn[k], start=(k == 0), stop=(k == NKC - 1))
                        act(tokT[blk * 4 + mi], bk, AF.Gelu_apprx_tanh)
                for blk in range(3):
                    w = wload_std(win, 0, 16, 1536 + blk * 512, 512)
                    for sub in range(4):
                        bk = nb()
                        for k in range(NKC):
                            mm(bk, hn[k][:, sub * 128:(sub + 1) * 128], w[:, k, :], start=(k == 0), stop=(k == NKC - 1))
                        act(vf[sub][:, blk * 512:(blk + 1) * 512], bk, AF.Gelu_apprx_tanh)
                w = wload_std(win, 0, 16, 3072, 512)
                for mi in range(4):
                    bk = nb()
                    for k in range(NKC):
                        mm(bk, w[:, k, mi * 128:(mi + 1) * 128], hn[k], start=(k == 0), stop=(k == NKC - 1))
                    evac_copy(qmT[mi], bk)
                while c.cc_pending:
                    c.cc_pending.pop(0)()
                stage(6 if tt == 0 else 13)
                for sub in range(4):
                    s6 = stt[sub]
                    for j in range(3):
                        P.op("dve", lambda e, o=st_t[:, sub, j * 6:(j + 1) * 6], i=vf[sub].ap[:, j * 512:(j + 1) * 512]: e.bn_stats(o, i),
                             reads=[vf[sub]], writes=[s6])
                    P.op("dve", lambda e, o=st_t[:, sub, 24:26], i=st_t[:, sub, 0:18]: e.bn_aggr(o, i), reads=[s6], writes=[s6])
                    P.op("act", lambda e, o=st_t[:, sub, 26:27], i=st_t[:, sub, 25:26]: e.activation(out=o, in_=i, func=AF.Sqrt, scale=1.0, bias=float(EPS)),
                         reads=[s6], writes=[s6])
                    P.op("dve", lambda e, o=st_t[:, sub, 26:27], i=st_t[:, sub, 26:27]: e.reciprocal(o, i), reads=[s6], writes=[s6])
                    P.op("dve", lambda e, o=vn[sub].ap, i=vf[sub].ap, m_=st_t[:, sub, 24:25], r_=st_t[:, sub, 26:27]:
                         e.tensor_scalar(o, i, m_, r_, ALU.subtract, ALU.mult), reads=[vf[sub], s6], writes=[vn[sub]])
                stage(7 if tt == 0 else 13)
                for g in range(12):
                    bk = nb()
                    for sub in range(4):
                        mm(bk[:, sub * 128:(sub + 1) * 128], vn[sub][:, g * 128:(g + 1) * 128], wsT[:, g, :], start=True, stop=True,
                           signal=(sub == 3))
                    t1 = ring(tmpf, "tmpn")
                    for sub in range(4):
                        vstt(t1[:, sub * 128:(sub + 1) * 128], bk[:, sub * 128:(sub + 1) * 128], gcol(G_LNG + g), Bt[:, g, :],
                             ALU.mult, ALU.add, reads=[gv])
                    vtt(tokT[g], t1, tokT[g], ALU.mult)
                stage(8 if tt == 0 else 13)
                mem_attention()
                stage(9 if tt == 0 else 13)
                out_proj_and_residual(a_w_out[0], tokT + memoT, G_NORM(0, 1))
                stage(10 if tt == 0 else 13)
                ffn(0)
                stage(11 if tt == 0 else 13)
                rms_to(h, hn, G_KVSRC, D)
                w = wload_std(w_kv_a, 0, 16, 0, 512)
                sbk = STAT
                bks_ = mm4_kmajor(w, hn)
                for mi in range(4):
                    bk = bks_[mi]
                    act(ckf[mi], bk, AF.Copy)
                    stat_accum(sbk, ckf[mi], mi == 0, mi == 3)
                r = make_rstd(sbk, 512)
                for mi in range(4):
                    vstt(ckT[mi], ckf[mi], gcol(G_KVN + mi), r, ALU.mult, ALU.mult, reads=[gv])
                srcs = [
                    (lambda t, s_: t[:, s_, 0:16 * 128].rearrange("p (k n) -> p k n", n=128)[:, :, 0:64],
                     w_kv_a[:, 512:576].rearrange("(k p) n -> p k n", p=128)),
                    (lambda t, s_: t[:, s_, 0:16 * 128].rearrange("p (k n) -> p k n", n=128)[:, :, 64:96],
                     w_kv_a[:, 544:576].rearrange("(k p) n -> p k n", p=128)),
                    (lambda t, s_: t[:, s_, 0:16 * 128].rearrange("p (k n) -> p k n", n=128)[:, :, 96:128],
                     w_kv_a[:, 512:544].rearrange("(k p) n -> p k n", p=128)),
                ]
                s_, slot = wload(srcs)
                wr = Tl(wview(s_, 16, 128), slot.bufs)
                ba = nb()
                bb = nb()
                for k in range(NKC):
                    mm(ba[0:64, :], wr[:, k, 0:64], hn[k], start=(k == 0), stop=(k == NKC - 1))
                for k in range(NKC):
                    mm(bb[0:64, :], wr[:, k, 64:128], hn[k], start=(k == 0), stop=(k == NKC - 1))
                kro = krs[tt % 2]
                rope_combine(kro, ba, bb)
                st1, st2 = [], []
                st1.append(Tl(None, [Buf()]))
                P.dma("sp", l1[tt][1536:1600, :], kro.ap[0:64, :], reads=[kro], writes=[st1[-1]])
                for blk in range(3):
                    w = wload_std(w_uk, 0, 4, blk * 512, 512)
                    for mi in range(4):
                        hh = blk * 4 + mi
                        bk = nb()
                        for k in range(4):
                            mm(bk, w[:, k, mi * 128:(mi + 1) * 128], ckT[k], start=(k == 0), stop=(k == 3))
                        ks = kst[hh % 4]
                        evac_copy(ks, bk)
                        st1.append(Tl(None, [Buf()]))
                        P.dma("sp", l1[tt][hh * 128:(hh + 1) * 128, :], ks.ap, reads=[ks], writes=[st1[-1]])
                for blk in range(3):
                    w = wload_std(w_uv, 0, 4, blk * 512, 512)
                    for sub in range(4):
                        bk = nb()
                        for k in range(4):
                            mm(bk, ckT[k][:, sub * 128:(sub + 1) * 128], w[:, k, :], start=(k == 0), stop=(k == 3))
                        vs = kst[sub]
                        evac_copy(vs, bk)
                        st2.append(Tl(None, [Buf()]))
                        P.dma("sp", l2v[tt][sub * 128:(sub + 1) * 128, blk * 512:(blk + 1) * 512], vs.ap, reads=[vs],
                              writes=[st2[-1]])
                RG = [[0, 1], [2, 3], [4, 5], [6, 7]]

                def issue_cc(tt=tt, st1=st1, st2=st2):
                    P.custom("pool", lambda e, i_=l1[tt], o_=g1[tt]: e.collective_compute("AllGather", ALU.bypass, replica_groups=RG,
                                                                                         ins=[i_[:, :]], outs=[o_[:, :]]),
                             reads=st1, writes=[g1dep[tt]], sem=ccsems[2 * tt], amount=1)
                    P.custom("pool", lambda e, i_=l2[tt], o_=g2[tt]: e.collective_compute("AllGather", ALU.bypass, replica_groups=RG,
                                                                                         ins=[i_[:, :]], outs=[o_[:, :]]),
                             reads=st2, writes=[g2dep[tt]], sem=ccsems[2 * tt + 1], amount=1)
                c.cc_pending.append(issue_cc)
                stage(12 if tt == 0 else 13)
                for k in range(NKC):
                    P.dma("sp", hT_d[k * 128:(k + 1) * 128, t0:t0 + TT], h[k].ap, reads=[h[k]], writes=[hdep[tt][k]])

        while getattr(c, "cc_pending", []):
            c.cc_pending.pop(0)()
        if doB:
            setup_mem(1)
            for rk in range(2):
                for t_ in range(NTILE):
                    c0_ = (rk * NTILE + t_) * TT
                    P.dma("sp", krTs.ap[0:64, c0_:c0_ + TT], g1[t_][rk * 1600 + 1536:rk * 1600 + 1600, :], reads=[g1dep[t_]],
                          writes=[krTs])
            for tt in range(NTILE):
                j = tt
                t0 = tt * TT
                for k in range(NKC):
                    P.dma("sp", h[k].ap, hT_d[k * 128:(k + 1) * 128, t0:t0 + TT], reads=[hdep[tt][k]], writes=[h[k]])
                rms_to(h, hn, G_NORM(1, 0), D, split=True)
                tables(t0)
                w = wload_std(b_w_in[0], 0, 16, 0, 512)
                sbk = STAT
                bks_ = mm4_kmajor(w, hn)
                for mi in range(4):
                    bk = bks_[mi]
                    act(cqf[mi], bk, AF.Copy)
                    stat_accum(sbk, cqf[mi], mi == 0, mi == 3)
                r = make_rstd(sbk, 512)
                for mi in range(4):
                    vstt(cqT[mi], cqf[mi], gcol(G_BQ + mi), r, ALU.mult, ALU.mult, reads=[gv])
                w = wload_std(b_w_in[0], 0, 16, 512, 512)
                for mi in range(4):
                    bk = nb()
                    for k in range(NKC):
                        mm(bk, w[:, k, mi * 128:(mi + 1) * 128], hn[k], start=(k == 0), stop=(k == NKC - 1))
                    evac_copy(qmT[mi], bk)
                mem_attention()
                uq = b_w_uq[0]
                for half in range(2):
                    cbase = half * 1152
                    v3 = lambda t, s_: t[:, s_, 0:4 * 1536].rearrange("p (k n) -> p k n", n=1536)
                    parts = [
                        (lambda t, s_: v3(t, s_)[:, :, 0:1152], uq[:, cbase:cbase + 1152].rearrange("(k p) n -> p k n", p=128)),
                    ]
                    for kq in range(4):
                        rsrc = uq[kq * 128:(kq + 1) * 128, cbase:cbase + 1152].rearrange("p (hh d) -> p hh d", d=192)
                        parts.append((lambda t, s_, kq=kq: v3(t, s_)[:, kq, 1152:1536].rearrange("p (hh d) -> p hh d", d=64)[:, :, 0:32],
                                      rsrc[:, :, 160:192]))
                        parts.append((lambda t, s_, kq=kq: v3(t, s_)[:, kq, 1152:1536].rearrange("p (hh d) -> p hh d", d=64)[:, :, 32:64],
                                      rsrc[:, :, 128:160]))
                    s_, slot = wload(parts)
                    w = Tl(wview(s_, 4, 1536), slot.bufs)
                    for hl in range(6):
                        hh = half * 6 + hl
                        bk = nb()
                        for k in range(4):
                            mm(bk, w[:, k, hl * 192:hl * 192 + 128], cqT[k], start=(k == 0), stop=(k == 3))
                        evac_copy(qnT[hh], bk)
                        ba = nb()
                        bb = nb()
                        for k in range(4):
                            mm(ba[0:64, :], w[:, k, hl * 192 + 128:hl * 192 + 192], cqT[k], start=(k == 0), stop=(k == 3))
                        for k in range(4):
                            mm(bb[0:64, :], w[:, k, 1152 + hl * 64:1152 + (hl + 1) * 64], cqT[k], start=(k == 0), stop=(k == 3))
                        rope_combine(qrT[hh], ba, bb)
                nblk_half = 4 * j + 4
                c.ring = [0, 1, 2]
                c.bank = 0
                for hh in range(12):
                    ob = banks[3 + 2 * (hh % 2)]
                    sbk = banks[4 + 2 * (hh % 2)]
                    first = True
                    chunks = []
                    for rk in range(2):
                        b0 = 0
                        while b0 < nblk_half:
                            nbk = min(4, nblk_half - b0)
                            chunks.append((rk, b0, nbk))
                            b0 += nbk
                    nblocks_total = 2 * nblk_half
                    pend = []
                    st_ = {"first": True, "done": 0}

                    def pv_stage(item, ob=ob, sbk=sbk, st_=st_, nblocks_total=nblocks_total):
                        vc_, bi, p, c0 = item
                        st_["done"] += 1
                        last = (st_["done"] == nblocks_total)
                        mm(ob[:, c0:TT], vc_[:, bi, :], p[:, c0:TT], start=st_["first"], stop=last, signal=True)
                        mm(sbk[:, c0:TT], ones, p[:, c0:TT], start=st_["first"], stop=last, signal=True)
                        st_["first"] = False

                    for (rk, b0, nbk) in chunks:
                        i_ = c.kvn
                        c.kvn = (c.kvn + 1) % NKV
                        kc_, vc_ = Kc[i_], Vc[i_]
                        key0 = rk * TOKC + b0 * 128
                        t_ = b0 // 4
                        P.dma("sp", kc_.ap[:, 0:TT], g1[t_][rk * 1600 + hh * 128:rk * 1600 + (hh + 1) * 128, :], reads=[g1dep[t_]],
                              writes=[kc_])
                        P.dma("sp", vc_.ap[:, 0:nbk, :],
                              g2v[t_][rk][:, hh * 128:(hh + 1) * 128].rearrange("(b p) d -> p b d", p=128),
                              reads=[g2dep[t_]], writes=[vc_])
                        for bi in range(nbk):
                            i = b0 + bi
                            if i < 4 * j:
                                c0 = 0
                                lp = None
                            else:
                                lp = i - 4 * j
                                c0 = lp * 128
                            kg0 = key0 + bi * 128
                            bk = nb()
                            mm(bk[:, c0:TT], kc_[:, bi * 128:(bi + 1) * 128], qnT[hh][:, c0:TT], start=True, stop=False, signal=False)
                            mm(bk[:, c0:TT], krTs[0:64, kg0:kg0 + 128], qrT[hh][0:64, c0:TT], start=False, stop=True)
                            p = ring(pT, "pTn")
                            act(p[:, c0:TT], bk[:, c0:TT], AF.Exp, scale=MLA_SCALE)
                            if lp is not None:
                                vtt(p[:, c0:c0 + 128], p[:, c0:c0 + 128], dmask[:, rk * 128:(rk + 1) * 128], ALU.mult)
                            pend.append((vc_, bi, p, c0))
                            if len(pend) > 2:
                                pv_stage(pend.pop(0))
                    while pend:
                        pv_stage(pend.pop(0))
                    rc = ring(tmpf, "tmpn")
                    vrecip(rc, sbk)
                    vtt(tokT[hh], ob, rc, ALU.mult)
                c.ring = [0, 1, 2, 3, 4, 5, 6]
                c.bank = 0
                out_proj_and_residual(b_w_out[0], tokT + memoT, G_NORM(1, 1))
                ffn(1)
                for sub in range(4):
                    for kq in range(4):
                        bk = nb()
                        for kk in range(4):
                            k = kq * 4 + kk
                            tr(bk[:, kk * 128:(kk + 1) * 128], h[k][:, sub * 128:(sub + 1) * 128], signal=(kk == 3))
                        evac_copy(xs[sub][:, kq * 512:(kq + 1) * 512], bk)
                    P.dma("sp", out[t0 + sub * 128:t0 + (sub + 1) * 128, :], xs[sub].ap, reads=[xs[sub]])


    except StopBuild:
        if mode == "A":
            def dump(tile, rb, cb, np_=128, ncol=512, conv=True):
                if conv:
                    t_ = ring(tmpf, "tmpn")
                    vcopy(t_[0:np_, 0:ncol], tile)
                    src = t_[0:np_, 0:ncol]
                else:
                    src = tile
                P.dma("sp", hT_d[rb * 128:rb * 128 + np_, cb * 512:cb * 512 + ncol], src.ap, reads=[src])
            sp_ = stop
            for k in range(NKC):
                if sp_ >= 4:
                    dump(h[k], k, 0, conv=False)
                if 5 <= sp_ < 10:
                    dump(hn[k], k, 1)
            for k in range(12):
                if 6 <= sp_ < 10:
                    dump(tokT[k], k, 2)
            for k in range(4):
                if 9 <= sp_ < 10:
                    dump(memoT[k], 12 + k, 2)
            dump(gv, 0, 3, ncol=256, conv=False)
            if sp_ >= 12:
                dump(Ctab, 5, 3, np_=64, conv=False)
                dump(Stab, 6, 3, np_=64, conv=False)
    fw = {"sp": [(P.sp_sems[i], P.sp_cnt[i]) for i in range(len(P.sp_sems)) if P.sp_cnt[i] > 0]}
    P.emit(fw)
    es.close()
    return nc, P


def _host_consts(r):
    ident = np.eye(128, dtype=np.float32)
    triu = np.triu(np.ones((128, 128), np.float32))
    ones_ = np.ones((128, 128), np.float32)
    zeros_ = np.zeros((128, 128), np.float32)
    def mk(rk):
        if rk < r:
            return ones_
        if rk == r:
            return triu
        return zeros_
    return np.concatenate([ident, triu, mk(0), mk(1)], axis=1)


def _shard_tokens(a, b, r):
    s = a.reshape(32, 128, *a.shape[1:])
    return np.ascontiguousarray(s[r::2].reshape(TOKC, *a.shape[1:]))


_NC_CACHE = {}


def _get_nc(mode):
    if mode not in _NC_CACHE:
        _NC_CACHE[mode] = build(mode)[0]
    return _NC_CACHE[mode]


def kernel(x, mem, positions, norm_gains, mem_norm, w_mem_kv, ffn_w_gate, ffn_w_up, ffn_w_down,
           a_w_in, a_ln_g, a_ln_b, a_w_s, a_b_s, a_w_out, kv_src_norm, w_kv_a, kv_norm, w_uk, w_uv,
           b_w_in, b_q_norm, b_w_uq, b_w_out):
    f = lambda a: np.ascontiguousarray(np.asarray(a, dtype=np.float32))
    x = f(x); mem = f(mem)
    positions = np.ascontiguousarray(np.asarray(positions, dtype=np.int32))
    vecs = np.zeros((256, 128), np.float32)
    vecs[0:128] = f(norm_gains).reshape(128, 128)
    vecs[128:144] = f(kv_src_norm).reshape(16, 128)
    vecs[144:176] = f(mem_norm).reshape(32, 128)
    vecs[176:180] = f(b_q_norm).reshape(4, 128)
    vecs[180:184] = f(kv_norm).reshape(4, 128)
    vecs[184:196] = f(a_ln_b).reshape(12, 128)
    vecs[196:208] = f(a_ln_g).reshape(12, 128)
    inv = (10000.0 ** (-np.arange(0, 64, 2, dtype=np.float32) / 64)).astype(np.float32)
    invf = np.concatenate([inv, inv]).reshape(64, 1).astype(np.float32)
    common = dict(vecs=vecs, invf=invf)
    w_mem_kv = f(w_mem_kv); ffn_w_gate = f(ffn_w_gate); ffn_w_up = f(ffn_w_up); ffn_w_down = f(ffn_w_down)
    lw = lambda l: {"w_mem_kv%d" % l: w_mem_kv[l], "ffn_w_gate%d" % l: ffn_w_gate[l], "ffn_w_up%d" % l: ffn_w_up[l],
                    "ffn_w_down%d" % l: ffn_w_down[l]}
    wA = dict(**lw(0), a_w_in=f(a_w_in), a_w_s=f(a_w_s), a_b_s=f(a_b_s), a_w_out=f(a_w_out), w_kv_a=f(w_kv_a), w_uk=f(w_uk), w_uv=f(w_uv))
    wB = dict(**lw(1), b_w_in=f(b_w_in), b_w_uq=f(b_w_uq), b_w_out=f(b_w_out))
    cores = list(range(8))
    percore = []
    for cid in cores:
        b, r = cid // 2, cid % 2
        percore.append(dict(cst=_host_consts(r), pos=_shard_tokens(positions[b], b, r).reshape(1, TOKC),
                            mem=np.ascontiguousarray(mem[b])))
    ncAB = _get_nc("AB")
    mapsAB = []
    for cid in cores:
        b, r = cid // 2, cid % 2
        m = dict(common); m.update(wA); m.update(wB); m.update(percore[cid])
        m["x"] = _shard_tokens(x[b], b, r)
        mapsAB.append(m)
    resB = run_bass_kernel_spmd(ncAB, mapsAB, core_ids=cores).results
    outp = np.empty((4, 32, 128, D), np.float32)
    for cid in cores:
        b, r = cid // 2, cid % 2
        outp[b, r::2] = resB[cid]["out"].reshape(16, 128, D)
    return outp.reshape(4, SEQ, D)
```

```python
import math
from contextlib import ExitStack
import numpy as np
import ml_dtypes
import concourse.bass as bass
import concourse.mybir as mybir
from concourse.bass_utils import run_bass_kernel_spmd

F32 = mybir.dt.float32
BF16 = mybir.dt.bfloat16
I32 = mybir.dt.int32
AF = mybir.ActivationFunctionType
ALU = mybir.AluOpType

D = 2048
NKC = 16
TT = 512
NTILE = 4
TOKC = 2048
SEQ = 4096
DFF = 5632
NFC = 44
EPS = 1e-6
MLA_SCALE = 192 ** -0.5
MEM_SCALE = 128 ** -0.5
TWO_PI = 2.0 * math.pi


class Buf:
    __slots__ = ("w", "r")

    def __init__(self):
        self.w = None
        self.r = {}


class Tl:
    __slots__ = ("ap", "bufs")

    def __init__(self, ap, bufs):
        self.ap = ap
        self.bufs = bufs

    def __getitem__(self, key):
        return Tl(self.ap[key], self.bufs)


class Prog:
    ENGS = ("pe", "act", "dve", "pool", "sp")

    def __init__(self, nc, es, n_sp_sems=8):
        self.nc = nc
        self.ins = {e: [] for e in self.ENGS}
        self.cnt = {e: 0 for e in self.ENGS}
        self.known = {e: {} for e in self.ENGS}
        self.esem = {e: es.enter_context(nc.semaphore("sem_" + e)) for e in self.ENGS}
        self.sp_sems = [es.enter_context(nc.semaphore("spd%d" % i)) for i in range(n_sp_sems)]
        self.sp_cnt = [0] * n_sp_sems
        self.sp_next = 0
        self.n_ins = 0
        act_sems = [es.enter_context(nc.semaphore("actd%d" % i)) for i in range(4)]
        self.rings = {"sp": [self.sp_sems, self.sp_cnt, 0], "act": [act_sems, [0] * 4, 0]}

    def custom(self, eng, fn, reads, writes, sem, amount):
        waits = self._deps(eng, reads, writes)
        self.ins[eng].append((waits, fn, (sem, amount)))
        self._mark((sem, amount), reads, writes)
        self.n_ins += 1

    def _deps(self, eng, reads, writes):
        waits = {}
        known = self.known[eng]
        pe_sem = self.esem["pe"]

        def need(sem, val):
            if eng == "pe" and sem is pe_sem:
                return
            if known.get(sem, 0) >= val:
                return
            if waits.get(sem, 0) < val:
                waits[sem] = val

        for t in reads:
            for b in t.bufs:
                if b.w is not None:
                    need(*b.w)
        for t in writes:
            for b in t.bufs:
                if b.w is not None:
                    need(*b.w)
                for s, v in b.r.items():
                    need(s, v)
        for s, v in waits.items():
            known[s] = v
        return list(waits.items())

    def _mark(self, tok, reads, writes):
        s, v = tok
        for t in reads:
            for b in t.bufs:
                if b.r.get(s, 0) < v:
                    b.r[s] = v
        for t in writes:
            for b in t.bufs:
                b.w = tok
                b.r = {}

    def op(self, eng, fn, reads=(), writes=(), signal=True):
        waits = self._deps(eng, reads, writes)
        if signal:
            self.cnt[eng] += 1
            tok = (self.esem[eng], self.cnt[eng])
        else:
            tok = (self.esem[eng], self.cnt[eng] + 1)
        self.ins[eng].append((waits, fn, (self.esem[eng], 1) if signal else None))
        self._mark(tok, reads, writes)
        self.n_ins += 1
        return tok

    def dma(self, eng, out_ap, in_ap, reads=(), writes=(), sem=None, semcnt=None, chain=False):
        if sem is None:
            rg = self.rings[eng]
            j = rg[2]
            rg[2] = (j + 1) % len(rg[0])
            sem = rg[0][j]
            prev = rg[1][j]
            rg[1][j] += 16
            val = rg[1][j]
        else:
            prev = semcnt[0]
            semcnt[0] += 16
            val = semcnt[0]
        waits = dict(self._deps(eng, reads, writes))
        if (not chain) and prev > 0 and self.known[eng].get(sem, 0) < prev:
            waits[sem] = max(waits.get(sem, 0), prev)
            self.known[eng][sem] = prev
        self.ins[eng].append((list(waits.items()),
                              lambda e, o=out_ap, i=in_ap: e.dma_start(out=o, in_=i), (sem, 16)))
        tok = (sem, val)
        self._mark(tok, reads, writes)
        self.n_ins += 1
        return tok

    def emit(self, final_waits):
        nc = self.nc
        prog = self

        def run(eng_name, e):
            for waits, fn, inc in prog.ins[eng_name]:
                for s, v in waits:
                    e.wait_ge(s, v)
                ins = fn(e)
                if inc is not None:
                    ins.then_inc(inc[0], inc[1])
            for s, v in final_waits.get(eng_name, []):
                e.wait_ge(s, v)

        with nc.Block() as block:
            @block.tensor
            def _(e):
                run("pe", e)

            @block.scalar
            def _(e):
                run("act", e)

            @block.vector
            def _(e):
                run("dve", e)

            @block.gpsimd
            def _(e):
                run("pool", e)

            @block.sync
            def _(e):
                run("sp", e)


class Ctx:
    pass


class StopBuild(Exception):
    pass


def build(mode, stop=None):
    nc = bass.Bass("TRN2", target_bir_lowering=False)
    es = ExitStack()
    P = Prog(nc, es)
    c = Ctx()

    def stage(n):
        if stop is not None and n > stop:
            raise StopBuild()

    doA = mode in ("A", "AB")
    doB = mode in ("B", "AB")

    def din(name, shape, dt=F32):
        return nc.dram_tensor(name, list(shape), dt, kind="ExternalInput").ap()

    def dout(name, shape, dt=F32):
        return nc.dram_tensor(name, list(shape), dt, kind="ExternalOutput").ap()

    def dint(name, shape, dt=F32):
        return nc.dram_tensor(name, list(shape), dt, kind="Internal").ap()

    vecs = din("vecs", [256, 128])
    cst = din("cst", [128, 512])
    invf = din("invf", [64, 1])
    pos = din("pos", [1, TOKC], I32)
    mem = din("mem", [256, D])
    layers = ([0] if doA else []) + ([1] if doB else [])
    w_mem_kv = {l: din("w_mem_kv%d" % l, [D, 1024]) for l in layers}
    ffn_w_gate = {l: din("ffn_w_gate%d" % l, [D, DFF]) for l in layers}
    ffn_w_up = {l: din("ffn_w_up%d" % l, [D, DFF]) for l in layers}
    ffn_w_down = {l: din("ffn_w_down%d" % l, [DFF, D]) for l in layers}
    if doA:
        x = din("x", [TOKC, D])
        a_w_in = din("a_w_in", [1, D, 3584])
        a_w_s = din("a_w_s", [1, 12, 128, 128])
        a_b_s = din("a_b_s", [1, 12, 128])
        a_w_out = din("a_w_out", [1, D, D])
        w_kv_a = din("w_kv_a", [D, 576])
        w_uk = din("w_uk", [512, 1536])
        w_uv = din("w_uv", [512, 1536])
    if doB:
        b_w_in = din("b_w_in", [1, D, 1024])
        b_w_uq = din("b_w_uq", [1, 512, 2304])
        b_w_out = din("b_w_out", [1, D, D])
        out = dout("out", [TOKC, D])
    if mode == "A":
        hT_d = dout("hT", [D, TOKC])
        KT_loc = dout("KT", [12, 128, TOKC], BF16)
        krT_loc = dout("krT", [64, TOKC], BF16)
        V_loc = dout("V", [TOKC, 1536], BF16)
    elif mode == "B":
        hT_d = din("hT", [D, TOKC])
        KT_full = din("KT", [12, 128, SEQ], BF16)
        krT_full = din("krT", [64, SEQ], BF16)
        V_full = din("V", [SEQ, 1536], BF16)
    else:
        hT_d = dint("hT_scr", [D, TOKC])
        l1 = [dint("l1_%d" % t, [1600, TT], BF16) for t in range(NTILE)]
        l2 = [dint("l2_%d" % t, [1536, TT], BF16) for t in range(NTILE)]
        g1 = [dint("g1_%d" % t, [2 * 1600, TT], BF16) for t in range(NTILE)]
        g2 = [dint("g2_%d" % t, [2 * 1536, TT], BF16) for t in range(NTILE)]
        l2v = [a_.rearrange("r c -> (r c)").rearrange("(t d) -> t d", d=1536) for a_ in l2]
        g2v = [[g2[t][rk * 1536:(rk + 1) * 1536, :].rearrange("r c -> (r c)").rearrange("(t d) -> t d", d=1536)
                for rk in range(2)] for t in range(NTILE)]
        ccsems = [es.enter_context(nc.semaphore("ccs%d" % i)) for i in range(2 * NTILE)]
        g1dep = [Tl(None, [Buf()]) for _ in range(NTILE)]
        g2dep = [Tl(None, [Buf()]) for _ in range(NTILE)]
    if mode == "B":
        KTg = [KT_full[:, :, rk * TOKC:(rk + 1) * TOKC] for rk in range(2)]
        krTg = [krT_full[:, rk * TOKC:(rk + 1) * TOKC] for rk in range(2)]
        Vg = [V_full[rk * TOKC:(rk + 1) * TOKC, :] for rk in range(2)]
    kvst_bufs = []
    kvg_dep = Tl(None, [Buf()])
    hdep = [[Tl(None, [Buf()]) for _ in range(NKC)] for _ in range(NTILE)]

    def sb(name, shape, dt):
        return es.enter_context(nc.sbuf_tensor("sb_" + name, list(shape), dt))

    def gran(n):
        return [Buf() for _ in range(n)]

    hA = sb("hA", [128, NKC, TT], F32)
    h = [Tl(hA[:, k, :], [Buf()]) for k in range(NKC)]

    regB = sb("regB", [128, 8192], F32)
    gB = gran(32)
    regC = sb("regC", [128, 11264], F32)
    gC = gran(45)

    def view(reg, grans, f0, f1, dt, inner):
        ap = reg[:, f0:f1]
        if dt == BF16:
            ap = ap.bitcast(BF16)
            nel = (f1 - f0) * 2
            bpe = 2
        else:
            nel = f1 - f0
            bpe = 4
        n = nel // inner
        ap3 = ap.rearrange("p (k n) -> p k n", n=inner)
        tiles = []
        for k in range(n):
            b0 = (f0 * 4 + k * inner * bpe) // 1024
            b1 = (f0 * 4 + (k + 1) * inner * bpe - 1) // 1024
            tiles.append(Tl(ap3[:, k, :], grans[b0:b1 + 1]))
        return tiles

    hn = view(regB, gB, 0, 4096, BF16, TT)
    tokT = view(regB, gB, 4096, 7168, BF16, TT)
    memoT = view(regB, gB, 7168, 8192, BF16, TT)
    fT = view(regB, gB, 0, 8192, F32, TT)
    vf = view(regC, gC, 0, 6144, F32, 1536)
    vn = view(regC, gC, 6144, 9216, BF16, 1536)
    qmT = view(regC, gC, 9216, 10240, BF16, TT)
    mixT = view(regC, gC, 0, 8192, F32, TT)
    actT = view(regC, gC, 0, 11264, BF16, TT)
    xs = view(regC, gC, 0, 8192, F32, D)
    cqT = view(regC, gC, 8192, 9216, BF16, TT)
    qnT = view(regC, gC, 0, 3072, BF16, TT)
    qrT = view(regC, gC, 3072, 6144, BF16, TT)
    cqf = view(regC, gC, 6144, 8192, F32, TT)
    ckf = view(regC, gC, 0, 2048, F32, TT)
    ckT = view(regC, gC, 2048, 3072, BF16, TT)
    kst = view(regC, gC, 3072, 4096, BF16, TT)
    vst = view(regC, gC, 4096, 5632, BF16, 1536)
    krs = view(regC, gC, 5632, 6144, BF16, TT)
    memst = view(regC, gC, 0, 4096, F32, D)
    memTf = view(regC, gC, 4096, 6144, F32, 256)
    memTf2 = view(regC, gC, 6144, 8192, F32, 256)
    memTf = memTf + memTf2
    memnT = view(regC, gC, 8192, 10240, BF16, 256)

    NS = 3
    wslot_t = sb("wslots", [128, NS, 8192], BF16)
    wslots = [Tl(wslot_t[:, s, :], [Buf()]) for s in range(NS)]
    wsem = [es.enter_context(nc.semaphore("wsem%d" % s)) for s in range(NS)]
    wcnt = [[0] for _ in range(NS)]
    c.wnext = 0
    c.cc_pending = []
    cc_sem = es.enter_context(nc.semaphore("cc_sem"))

    cstf_t = sb("cstf", [128, 256], F32)
    cstf = Tl(cstf_t[:, :], [Buf()])
    ident = cstf[:, 0:128]
    triu = cstf[:, 128:256]
    dmask_t = sb("dmask", [128, 256], BF16)
    dmask = Tl(dmask_t[:, :], [Buf()])
    ones_t = sb("ones", [128, 128], BF16)
    ones = Tl(ones_t[:, :], [Buf()])
    gv_t = sb("gv", [128, 256], F32)
    gv = Tl(gv_t[:, :], [Buf()])
    invf_t = sb("invf", [64, 1], F32)
    invfT = Tl(invf_t[:, :], [Buf()])
    sq_t = sb("sq", [128, 4, TT], BF16)
    sq = [Tl(sq_t[:, i, :], [Buf()]) for i in range(4)]
    c.sqn = 0
    tmp_t = sb("tmpf", [128, 4, TT], F32)
    tmpf = [Tl(tmp_t[:, i, :], [Buf()]) for i in range(4)]
    c.tmpn = 0
    rstd_t = sb("rstd", [128, 2, TT], F32)
    rstd = [Tl(rstd_t[:, i, :], [Buf()]) for i in range(2)]
    c.rstdn = 0
    pT_t = sb("pT", [128, 4, TT], BF16)
    pT = [Tl(pT_t[:, i, :], [Buf()]) for i in range(4)]
    c.pTn = 0
    KmT_t = sb("KmT", [128, 4, 256], BF16)
    KmT = Tl(KmT_t[:, :, :], [Buf()])
    Vm_t = sb("Vm", [128, 2, 512], BF16)
    Vm = Tl(Vm_t[:, :, :], [Buf()])
    tab_t = sb("tabs", [64, 2, TT], F32)
    Ctab = Tl(tab_t[:, 0, :], [Buf()])
    Stab = Tl(tab_t[:, 1, :], [Buf()])
    st_t = sb("stats", [128, 4, 32], F32)
    stt = [Tl(st_t[:, i, :], [Buf()]) for i in range(4)]
    regD = sb("regD", [128, 5120], F32)
    gD = gran(20)
    if doA:
        wsT = Tl(regD[:, 0:768].bitcast(BF16).rearrange("p (g t) -> p g t", t=128), gD[0:3])
        Bt = Tl(regD[:, 768:2304].rearrange("p (g t) -> p g t", t=128), gD[3:9])
        bsb = view(regC, gC, 8192, 9728, F32, 1536)[0]
    if doB:
        krTs = Tl(regD[:, 0:2048].bitcast(BF16), gD[0:8])
        NKV = 6
        Kc = [Tl(regD[:, 2048 + i * 256:2048 + (i + 1) * 256].bitcast(BF16), gD[8 + i:9 + i]) for i in range(NKV)]
        Vc = [Tl(regD[:, 3584 + i * 256:3584 + (i + 1) * 256].bitcast(BF16).rearrange("p (b d) -> p b d", d=128),
                 gD[14 + i:15 + i]) for i in range(NKV)]
        c.kvn = 0

    banks = []
    for i in range(8):
        t = es.enter_context(nc.psum_tensor("ps%d" % i, [128, 512], F32))
        banks.append(Tl(t[:, :], [Buf()]))
    c.bank = 0
    c.ring = [0, 1, 2, 3, 4, 5, 6]
    STAT = banks[7]

    def nb():
        c.bank = (c.bank + 1) % len(c.ring)
        return banks[c.ring[c.bank]]

    def ring(lst, attr):
        i = getattr(c, attr)
        setattr(c, attr, (i + 1) % len(lst))
        return lst[i]

    def mm(out, lhsT, rhs, start, stop, signal=None):
        if signal is None:
            signal = stop
        P.op("pe", lambda e, o=out.ap, l=lhsT.ap, r=rhs.ap, s=start, t=stop: e.matmul(o, l, r, start=s, stop=t),
             reads=[lhsT, rhs], writes=[out], signal=signal)

    def tr(out, in_, signal=True):
        P.op("pe", lambda e, o=out.ap, i=in_.ap, d=ident.ap: e.transpose(o, i, d),
             reads=[in_, ident], writes=[out], signal=signal)

    def act(out, in_, func, scale=1.0, reads=(), bias=None):
        if bias is None:
            P.op("act", lambda e, o=out.ap, i=in_.ap, f=func, s=scale: e.activation(out=o, in_=i, func=f, scale=s),
                 reads=[in_] + list(reads), writes=[out])
        else:
            P.op("act", lambda e, o=out.ap, i=in_.ap, f=func, s=scale, b=bias: e.activation(out=o, in_=i, func=f, scale=s, bias=b),
                 reads=[in_] + list(reads), writes=[out])

    def vts(out, in0, s1, s2, op0, op1, reads=()):
        if s2 is None:
            P.op("dve", lambda e, o=out.ap, i=in0.ap: e.tensor_scalar(o, i, s1, None, op0),
                 reads=[in0] + list(reads), writes=[out])
        else:
            P.op("dve", lambda e, o=out.ap, i=in0.ap: e.tensor_scalar(o, i, s1, s2, op0, op1),
                 reads=[in0] + list(reads), writes=[out])

    def vstt(out, in0, scalar, in1, op0, op1, reads=()):
        P.op("dve", lambda e, o=out.ap, i0=in0.ap, i1=in1.ap: e.scalar_tensor_tensor(o, i0, scalar, i1, op0, op1),
             reads=[in0, in1] + list(reads), writes=[out])

    def vtt(out, in0, in1, op):
        P.op("dve", lambda e, o=out.ap, i0=in0.ap, i1=in1.ap: e.tensor_tensor(o, i0, i1, op),
             reads=[in0, in1], writes=[out])

    def vcopy(out, in_):
        P.op("dve", lambda e, o=out.ap, i=in_.ap: e.tensor_copy(o, i), reads=[in_], writes=[out])

    def vrecip(out, in_):
        P.op("dve", lambda e, o=out.ap, i=in_.ap: e.reciprocal(o, i), reads=[in_], writes=[out])

    c.cp = 0

    def evac_copy(out, in_):
        c.cp ^= 1
        if c.cp:
            act(out, in_, AF.Copy)
        else:
            vcopy(out, in_)

    def wload(parts):
        s = c.wnext
        c.wnext = (s + 1) % NS
        slot = wslots[s]
        for pi, (dst_ap, src_ap) in enumerate([(p[0](wslot_t, s), p[1]) for p in parts]):
            P.dma("pool", dst_ap, src_ap, reads=(), writes=[slot] if pi == 0 else [], sem=wsem[s], semcnt=wcnt[s],
                  chain=(pi > 0))
        slot.bufs[0].w = (wsem[s], wcnt[s][0])
        return s, slot

    def wview(s, nk, ncol):
        return wslot_t[:, s, 0:nk * ncol].rearrange("p (k n) -> p k n", n=ncol)

    def wload_std(wmat, k0, nk, c0, ncol):
        src = wmat[k0 * 128:(k0 + nk) * 128, c0:c0 + ncol].rearrange("(k p) n -> p k n", p=128)
        s, slot = wload([(lambda t, s_, nk=nk, ncol=ncol: t[:, s_, 0:nk * ncol].rearrange("p (k n) -> p k n", n=ncol), src)])
        return Tl(wview(s, nk, ncol), slot.bufs)

    gcol = lambda j: gv_t[:, j:j + 1]

    def stat_accum(stat_bank, src, first, last):
        s = ring(sq, "sqn")
        act(s, src, AF.Square)
        mm(stat_bank, ones, s, start=first, stop=last, signal=True)

    def make_rstd(stat_bank, dim):
        r = ring(rstd, "rstdn")
        act(r, stat_bank, AF.Sqrt, scale=1.0 / dim, bias=float(EPS))
        vrecip(r, r)
        return r

    def mm4_kmajor(w, rhs):
        bks = [nb() for _ in range(4)]
        nk = len(rhs)
        for k in range(nk):
            for mi in range(4):
                mm(bks[mi], w[:, k, mi * 128:(mi + 1) * 128], rhs[k], start=(k == 0), stop=(k == nk - 1),
                   signal=(k == nk - 1))
        return bks

    def rms_to(srcs, dsts, gj, dim, split=False):
        sbk = STAT
        n = len(srcs)
        for k in range(n):
            if split and (k % 2 == 1):
                s_ = ring(sq, "sqn")
                vtt(s_, srcs[k], srcs[k], ALU.mult)
                mm(sbk, ones, s_, start=(k == 0), stop=(k == n - 1), signal=True)
            else:
                stat_accum(sbk, srcs[k], k == 0, k == n - 1)
        r = make_rstd(sbk, dim)
        for k in range(n):
            vstt(dsts[k], srcs[k], gcol(gj + k), r, ALU.mult, ALU.mult, reads=[gv])

    def residual_add(srcs, r, gj):
        for k in range(NKC):
            vstt(srcs[k], srcs[k], gcol(gj + k), r, ALU.mult, ALU.mult, reads=[gv])
            vtt(h[k], h[k], srcs[k], ALU.add)

    try:
        P.dma("sp", cstf.ap, cst[:, 0:256], writes=[cstf])
        P.dma("pool", dmask.ap, cst[:, 256:512], writes=[dmask], sem=wsem[0], semcnt=wcnt[0])
        P.dma("sp", invfT.ap, invf[:, :], writes=[invfT])
        P.op("dve", lambda e: e.memset(ones_t[:, :], 1.0), writes=[ones])
        for half in range(2):
            st = xs[half][:, 0:128]
            P.dma("sp", st.ap, vecs[half * 128:(half + 1) * 128, :], writes=[st])
            bk = nb()
            tr(bk[:, 0:128], st)
            vcopy(gv[:, half * 128:(half + 1) * 128], bk[:, 0:128])
        stage(1)
        G_NORM = lambda l, n: l * 64 + n * 16
        G_KVSRC = 128
        G_MEM = lambda l: 144 + l * 16
        G_BQ = 176
        G_KVN = 180
        G_LNB = 184
        G_LNG = 196

        def setup_mem(l):
            for t in range(2):
                P.dma("sp", memst[t].ap, mem[t * 128:(t + 1) * 128, :], writes=[memst[t]])
            for k in range(NKC):
                bk = nb()
                for t in range(2):
                    tr(bk[:, t * 128:(t + 1) * 128], memst[t][:, k * 128:(k + 1) * 128], signal=(t == 1))
                evac_copy(memTf[k], bk[:, 0:256])
            sbk = STAT
            for k in range(NKC):
                s = ring(sq, "sqn")
                act(s[:, 0:256], memTf[k], AF.Square)
                mm(sbk[:, 0:256], ones, s[:, 0:256], start=(k == 0), stop=(k == NKC - 1), signal=True)
            r = ring(rstd, "rstdn")
            act(r[:, 0:256], sbk[:, 0:256], AF.Sqrt, scale=1.0 / D, bias=float(EPS))
            vrecip(r[:, 0:256], r[:, 0:256])
            for k in range(NKC):
                vstt(memnT[k], memTf[k], gcol(G_MEM(l) + k), r[:, 0:256], ALU.mult, ALU.mult, reads=[gv])
            wk = wload_std(w_mem_kv[l], 0, 16, 0, 512)
            for hh in range(4):
                bk = nb()
                for k in range(NKC):
                    mm(bk[:, 0:256], wk[:, k, hh * 128:(hh + 1) * 128], memnT[k], start=(k == 0), stop=(k == NKC - 1))
                evac_copy(KmT[:, hh, :], bk[:, 0:256])
            wv = wload_std(w_mem_kv[l], 0, 16, 512, 512)
            for mt in range(2):
                bk = nb()
                for k in range(NKC):
                    mm(bk, memnT[k][:, mt * 128:(mt + 1) * 128], wv[:, k, :], start=(k == 0), stop=(k == NKC - 1))
                evac_copy(Vm[:, mt, :], bk)

        def mem_attention():
            def pv(hh, pts):
                ob = nb()
                sbk = nb()
                for mt in range(2):
                    mm(ob, Vm[:, mt, hh * 128:(hh + 1) * 128], pts[mt], start=(mt == 0), stop=(mt == 1))
                for mt in range(2):
                    mm(sbk, ones, pts[mt], start=(mt == 0), stop=(mt == 1))
                rc = ring(tmpf, "tmpn")
                vrecip(rc, sbk)
                vtt(memoT[hh], ob, rc, ALU.mult)

            prev = None
            for hh in range(4):
                pts = []
                for mt in range(2):
                    bk = nb()
                    mm(bk, KmT[:, hh, mt * 128:(mt + 1) * 128], qmT[hh], start=True, stop=True)
                    p = ring(pT, "pTn")
                    act(p, bk, AF.Exp, scale=MEM_SCALE)
                    pts.append(p)
                if prev is not None:
                    pv(*prev)
                prev = (hh, pts)
            pv(*prev)

        def tables(t0):
            pt_ = ring(tmpf, "tmpn")
            posi = Tl(pt_.ap[0:64, :].bitcast(I32), pt_.bufs)
            P.dma("sp", posi.ap, pos[0, t0:t0 + TT].partition_broadcast(64), writes=[posi])
            a0 = ring(tmpf, "tmpn")[0:64, :]
            a1 = ring(tmpf, "tmpn")[0:64, :]
            C1 = 6.28125
            C2 = TWO_PI - 6.28125
            vcopy(a0, posi)
            vts(a0, a0, invf_t[:, 0:1], None, ALU.mult, None, reads=[invfT])
            vts(a1, a0, float(1.0 / TWO_PI), None, ALU.mult, None)
            ki = Tl(posi.ap, posi.bufs)
            vcopy(ki, a1)
            vcopy(a1, ki)
            vstt(a0, a1, float(-C1), a0, ALU.mult, ALU.add)
            vstt(a0, a1, float(-C2), a0, ALU.mult, ALU.add)
            vts(a1, a0, float(math.pi / 2), float(-TWO_PI), ALU.is_gt, ALU.mult)
            vstt(a1, a0, float(math.pi / 2), a1, ALU.add, ALU.add)
            PI_ = 3.1415925
            vts(a1, a1, PI_, -PI_, ALU.min, ALU.max)
            vts(a0, a0, PI_, -PI_, ALU.min, ALU.max)
            act(Ctab, a1, AF.Sin, scale=1.0)
            act(Stab[0:32, :], a0[0:32, :], AF.Sin, scale=-1.0)
            act(Stab[32:64, :], a0[32:64, :], AF.Sin, scale=1.0)

        def rope_combine(dst, bk_a, bk_b):
            t1 = ring(tmpf, "tmpn")
            t2 = ring(tmpf, "tmpn")
            vtt(t1[0:64, :], bk_a[0:64, :], Ctab, ALU.mult)
            vtt(t2[0:64, :], bk_b[0:64, :], Stab, ALU.mult)
            vtt(dst[0:64, :], t1[0:64, :], t2[0:64, :], ALU.add)

        def out_proj_and_residual(wmat, rhs_list, gj):
            sbk = STAT
            pend_ = None
            for blk in range(4):
                w = wload_std(wmat, 0, 16, blk * 512, 512)
                for mi in range(4):
                    m = blk * 4 + mi
                    bk = nb()
                    for k in range(NKC):
                        mm(bk, w[:, k, mi * 128:(mi + 1) * 128], rhs_list[k], start=(k == 0), stop=(k == NKC - 1))
                    act(mixT[m], bk, AF.Copy)
                    s_ = ring(sq, "sqn")
                    act(s_, mixT[m], AF.Square)
                    if pend_ is not None:
                        mm(sbk, ones, pend_[1], start=(pend_[0] == 0), stop=False, signal=True)
                    pend_ = (m, s_)
            mm(sbk, ones, pend_[1], start=False, stop=True, signal=True)
            r = make_rstd(sbk, D)
            residual_add(mixT, r, gj)

        def ffn(l):
            rms_to(h, hn, G_NORM(l, 2), D)
            wg_m, wu_m, wd_m = ffn_w_gate[l], ffn_w_up[l], ffn_w_down[l]
            for blk in range(11):
                wg = wload_std(wg_m, 0, 16, blk * 512, 512)
                wu = wload_std(wu_m, 0, 16, blk * 512, 512)
                sgs = []
                bgs_ = mm4_kmajor(wg, hn) if blk == 0 else None
                for mi in range(4):
                    if bgs_ is not None:
                        bg = bgs_[mi]
                    else:
                        bg = nb()
                        for k in range(NKC):
                            mm(bg, wg[:, k, mi * 128:(mi + 1) * 128], hn[k], start=(k == 0), stop=(k == NKC - 1))
                    sg = ring(tmpf, "tmpn")
                    act(sg, bg, AF.Silu)
                    sgs.append(sg)
                for mi in range(4):
                    m = blk * 4 + mi
                    bu = nb()
                    for k in range(NKC):
                        mm(bu, wu[:, k, mi * 128:(mi + 1) * 128], hn[k], start=(k == 0), stop=(k == NKC - 1))
                    vtt(actT[m], sgs[mi], bu, ALU.mult)
            sbk = STAT
            dpend = []
            for blk in range(4):
                bks = [nb() for _ in range(4)]
                for kg in range(3):
                    nk = 16 if kg < 2 else 12
                    w = wload_std(wd_m, kg * 16, nk, blk * 512, 512)
                    for kl in range(nk):
                        k = kg * 16 + kl
                        for mi in range(4):
                            last_of_slot = (kl == nk - 1 and mi == 3)
                            stop = (k == NFC - 1)
                            mm(bks[mi], w[:, kl, mi * 128:(mi + 1) * 128], actT[k], start=(k == 0), stop=stop,
                               signal=(stop or last_of_slot))
                    if kg == 0:
                        for (m_, s_) in dpend:
                            mm(sbk, ones, s_, start=(m_ == 0), stop=False, signal=True)
                        dpend = []
                for mi in range(4):
                    m = blk * 4 + mi
                    act(fT[m], bks[mi], AF.Copy)
                    s_ = ring(sq, "sqn")
                    act(s_, fT[m], AF.Square)
                    dpend.append((m, s_))
            for (m_, s_) in dpend:
                mm(sbk, ones, s_, start=False, stop=(m_ == NKC - 1), signal=True)
            r = make_rstd(sbk, D)
            residual_add(fT, r, G_NORM(l, 3))

        if doA:
            for g in range(12):
                st = xs[0][:, g * 128:(g + 1) * 128]
                P.dma("sp", st.ap, a_w_s[0, g, :, :], writes=[st])
            P.dma("sp", bsb.ap, a_b_s[0, :, :].rearrange("g t -> (g t)").partition_broadcast(128), writes=[bsb])
            for g in range(12):
                bk = nb()
                tr(bk[:, 0:128], xs[0][:, g * 128:(g + 1) * 128])
                vtt(wsT[:, g, :], bk[:, 0:128], triu, ALU.mult)
            for g in range(12):
                bk = nb()
                mm(bk[:, 0:128], ones, wsT[:, g, :], start=True, stop=True)
                vstt(Bt[:, g, :], bk[:, 0:128], gcol(G_LNB + g), bsb[:, g * 128:(g + 1) * 128], ALU.mult, ALU.add, reads=[gv])
            stage(2)
            setup_mem(0)
            stage(3)

            for tt in range(NTILE):
                t0 = tt * TT
                for sub in range(4):
                    P.dma("act", xs[sub].ap, x[t0 + sub * 128:t0 + (sub + 1) * 128, :], writes=[xs[sub]])
                for sub in range(4):
                    for kq in range(4):
                        bk = nb()
                        for kk in range(4):
                            k = kq * 4 + kk
                            tr(bk[:, kk * 128:(kk + 1) * 128], xs[sub][:, k * 128:(k + 1) * 128], signal=(kk == 3))
                        o_tl = Tl(hA[:, kq * 4:(kq + 1) * 4, sub * 128:(sub + 1) * 128],
                                  [h[kq * 4 + kk].bufs[0] for kk in range(4)])
                        i_tl = Tl(bk.ap.rearrange("p (a b) -> p a b", b=128), bk.bufs)
                        evac_copy(o_tl, i_tl)
                stage(4 if tt == 0 else 13)
                rms_to(h, hn, G_NORM(0, 0), D, split=True)
                tables(t0)
                stage(5 if tt == 0 else 13)
                win = a_w_in[0]
                for blk in range(3):
                    w = wload_std(win, 0, 16, 1536 + blk * 512, 512)
                    if blk == 0:
                        bks_ = [nb() for _ in range(4)]
                        for k in range(NKC):
                            for sub in range(4):
                                mm(bks_[sub], hn[k][:, sub * 128:(sub + 1) * 128], w[:, k, :], start=(k == 0), stop=(k == NKC - 1),
                                   signal=(k == NKC - 1))
                        for sub in range(4):
                            act(vf[sub][:, 0:512], bks_[sub], AF.Gelu_apprx_tanh)
                        continue
                    for sub in range(4):
                        bk = nb()
                        for k in range(NKC):
                            mm(bk, hn[k][:, sub * 128:(sub + 1) * 128], w[:, k, :], start=(k == 0), stop=(k == NKC - 1))
                        act(vf[sub][:, blk * 512:(blk + 1) * 512], bk, AF.Gelu_apprx_tanh)
                for sub in range(4):
                    s6 = stt[sub]
                    for j in range(3):
                        P.op("dve", lambda e, o=st_t[:, sub, j * 6:(j + 1) * 6], i=vf[sub].ap[:, j * 512:(j + 1) * 512]: e.bn_stats(o, i),
                             reads=[vf[sub]], writes=[s6])
                    P.op("dve", lambda e, o=st_t[:, sub, 24:26], i=st_t[:, sub, 0:18]: e.bn_aggr(o, i), reads=[s6], writes=[s6])
                    P.op("act", lambda e, o=st_t[:, sub, 26:27], i=st_t[:, sub, 25:26]: e.activation(out=o, in_=i, func=AF.Sqrt, scale=1.0, bias=float(EPS)),
                         reads=[s6], writes=[s6])
                    P.op("dve", lambda e, o=st_t[:, sub, 26:27], i=st_t[:, sub, 26:27]: e.reciprocal(o, i), reads=[s6], writes=[s6])
                    P.op("dve", lambda e, o=vn[sub].ap, i=vf[sub].ap, m_=st_t[:, sub, 24:25], r_=st_t[:, sub, 26:27]:
                         e.tensor_scalar(o, i, m_, r_, ALU.subtract, ALU.mult), reads=[vf[sub], s6], writes=[vn[sub]])
                for blk in range(3):
                    w = wload_std(win, 0, 16, blk * 512, 512)
                    for mi in range(4):
                        bk = nb()
                        for k in range(NKC):
                            mm(bk, w[:, k, mi * 128:(mi + 1) * 128], hn[k], start=(k == 0), stop=(k == NKC - 1))
                        act(tokT[blk * 4 + mi], bk, AF.Gelu_apprx_tanh)
                w = wload_std(win, 0, 16, 3072, 512)
                for mi in range(4):
                    bk = nb()
                    for k in range(NKC):
                        mm(bk, w[:, k, mi * 128:(mi + 1) * 128], hn[k], start=(k == 0), stop=(k == NKC - 1))
                    evac_copy(qmT[mi], bk)
                while c.cc_pending:
                    c.cc_pending.pop(0)()
                stage(6 if tt == 0 else 13)
                stage(7 if tt == 0 else 13)
                for g in range(12):
                    bk = nb()
                    for sub in range(4):
                        mm(bk[:, sub * 128:(sub + 1) * 128], vn[sub][:, g * 128:(g + 1) * 128], wsT[:, g, :], start=True, stop=True,
                           signal=(sub == 3))
                    t1 = ring(tmpf, "tmpn")
                    for sub in range(4):
                        vstt(t1[:, sub * 128:(sub + 1) * 128], bk[:, sub * 128:(sub + 1) * 128], gcol(G_LNG + g), Bt[:, g, :],
                             ALU.mult, ALU.add, reads=[gv])
                    vtt(tokT[g], t1, tokT[g], ALU.mult)
                stage(8 if tt == 0 else 13)
                mem_attention()
                stage(9 if tt == 0 else 13)
                out_proj_and_residual(a_w_out[0], tokT + memoT, G_NORM(0, 1))
                stage(10 if tt == 0 else 13)
                ffn(0)
                stage(11 if tt == 0 else 13)
                rms_to(h, hn, G_KVSRC, D)
                w = wload_std(w_kv_a, 0, 16, 0, 512)
                sbk = STAT
                bks_ = mm4_kmajor(w, hn)
                for mi in range(4):
                    bk = bks_[mi]
                    act(ckf[mi], bk, AF.Copy)
                    stat_accum(sbk, ckf[mi], mi == 0, mi == 3)
                r = make_rstd(sbk, 512)
                for mi in range(4):
                    vstt(ckT[mi], ckf[mi], gcol(G_KVN + mi), r, ALU.mult, ALU.mult, reads=[gv])
                srcs = [
                    (lambda t, s_: t[:, s_, 0:16 * 128].rearrange("p (k n) -> p k n", n=128)[:, :, 0:64],
                     w_kv_a[:, 512:576].rearrange("(k p) n -> p k n", p=128)),
                    (lambda t, s_: t[:, s_, 0:16 * 128].rearrange("p (k n) -> p k n", n=128)[:, :, 64:96],
                     w_kv_a[:, 544:576].rearrange("(k p) n -> p k n", p=128)),
                    (lambda t, s_: t[:, s_, 0:16 * 128].rearrange("p (k n) -> p k n", n=128)[:, :, 96:128],
                     w_kv_a[:, 512:544].rearrange("(k p) n -> p k n", p=128)),
                ]
                s_, slot = wload(srcs)
                wr = Tl(wview(s_, 16, 128), slot.bufs)
                ba = nb()
                bb = nb()
                for k in range(NKC):
                    mm(ba[0:64, :], wr[:, k, 0:64], hn[k], start=(k == 0), stop=(k == NKC - 1))
                for k in range(NKC):
                    mm(bb[0:64, :], wr[:, k, 64:128], hn[k], start=(k == 0), stop=(k == NKC - 1))
                kro = krs[tt % 2]
                rope_combine(kro, ba, bb)
                st1, st2 = [], []
                st1.append(Tl(None, [Buf()]))
                P.dma("sp", l1[tt][1536:1600, :], kro.ap[0:64, :], reads=[kro], writes=[st1[-1]])
                for blk in range(3):
                    w = wload_std(w_uk, 0, 4, blk * 512, 512)
                    for mi in range(4):
                        hh = blk * 4 + mi
                        bk = nb()
                        for k in range(4):
                            mm(bk, w[:, k, mi * 128:(mi + 1) * 128], ckT[k], start=(k == 0), stop=(k == 3))
                        ks = kst[hh % 4]
                        evac_copy(ks, bk)
                        st1.append(Tl(None, [Buf()]))
                        P.dma("sp", l1[tt][hh * 128:(hh + 1) * 128, :], ks.ap, reads=[ks], writes=[st1[-1]])
                for blk in range(3):
                    w = wload_std(w_uv, 0, 4, blk * 512, 512)
                    for sub in range(4):
                        bk = nb()
                        for k in range(4):
                            mm(bk, ckT[k][:, sub * 128:(sub + 1) * 128], w[:, k, :], start=(k == 0), stop=(k == 3))
                        vs = kst[sub]
                        evac_copy(vs, bk)
                        st2.append(Tl(None, [Buf()]))
                        P.dma("sp", l2v[tt][sub * 128:(sub + 1) * 128, blk * 512:(blk + 1) * 512], vs.ap, reads=[vs],
                              writes=[st2[-1]])
                RG = [[0, 1], [2, 3], [4, 5], [6, 7]]

                def issue_cc(tt=tt, st1=st1, st2=st2):
                    P.custom("pool", lambda e, i_=l1[tt], o_=g1[tt]: e.collective_compute("AllGather", ALU.bypass, replica_groups=RG,
                                                                                         ins=[i_[:, :]], outs=[o_[:, :]]),
                             reads=st1, writes=[g1dep[tt]], sem=ccsems[2 * tt], amount=1)
                    P.custom("pool", lambda e, i_=l2[tt], o_=g2[tt]: e.collective_compute("AllGather", ALU.bypass, replica_groups=RG,
                                                                                         ins=[i_[:, :]], outs=[o_[:, :]]),
                             reads=st2, writes=[g2dep[tt]], sem=ccsems[2 * tt + 1], amount=1)
                c.cc_pending.append(issue_cc)
                stage(12 if tt == 0 else 13)
                for k in range(NKC):
                    P.dma("sp", hT_d[k * 128:(k + 1) * 128, t0:t0 + TT], h[k].ap, reads=[h[k]], writes=[hdep[tt][k]])

        while getattr(c, "cc_pending", []):
            c.cc_pending.pop(0)()
        if doB:
            setup_mem(1)
            for rk in range(2):
                for t_ in range(NTILE):
                    c0_ = (rk * NTILE + t_) * TT
                    P.dma("sp", krTs.ap[0:64, c0_:c0_ + TT], g1[t_][rk * 1600 + 1536:rk * 1600 + 1600, :], reads=[g1dep[t_]],
                          writes=[krTs])
            for tt in range(NTILE):
                j = tt
                t0 = tt * TT
                for k in range(NKC):
                    P.dma("sp", h[k].ap, hT_d[k * 128:(k + 1) * 128, t0:t0 + TT], reads=[hdep[tt][k]], writes=[h[k]])
                rms_to(h, hn, G_NORM(1, 0), D, split=True)
                tables(t0)
                w = wload_std(b_w_in[0], 0, 16, 0, 512)
                sbk = STAT
                bks_ = mm4_kmajor(w, hn)
                for mi in range(4):
                    bk = bks_[mi]
                    act(cqf[mi], bk, AF.Copy)
                    stat_accum(sbk, cqf[mi], mi == 0, mi == 3)
                r = make_rstd(sbk, 512)
                for mi in range(4):
                    vstt(cqT[mi], cqf[mi], gcol(G_BQ + mi), r, ALU.mult, ALU.mult, reads=[gv])
                w = wload_std(b_w_in[0], 0, 16, 512, 512)
                for mi in range(4):
                    bk = nb()
                    for k in range(NKC):
                        mm(bk, w[:, k, mi * 128:(mi + 1) * 128], hn[k], start=(k == 0), stop=(k == NKC - 1))
                    evac_copy(qmT[mi], bk)
                mem_attention()
                uq = b_w_uq[0]
                for half in range(2):
                    cbase = half * 1152
                    v3 = lambda t, s_: t[:, s_, 0:4 * 1536].rearrange("p (k n) -> p k n", n=1536)
                    parts = [
                        (lambda t, s_: v3(t, s_)[:, :, 0:1152], uq[:, cbase:cbase + 1152].rearrange("(k p) n -> p k n", p=128)),
                    ]
                    for kq in range(4):
                        rsrc = uq[kq * 128:(kq + 1) * 128, cbase:cbase + 1152].rearrange("p (hh d) -> p hh d", d=192)
                        parts.append((lambda t, s_, kq=kq: v3(t, s_)[:, kq, 1152:1536].rearrange("p (hh d) -> p hh d", d=64)[:, :, 0:32],
                                      rsrc[:, :, 160:192]))
                        parts.append((lambda t, s_, kq=kq: v3(t, s_)[:, kq, 1152:1536].rearrange("p (hh d) -> p hh d", d=64)[:, :, 32:64],
                                      rsrc[:, :, 128:160]))
                    s_, slot = wload(parts)
                    w = Tl(wview(s_, 4, 1536), slot.bufs)
                    for hl in range(6):
                        hh = half * 6 + hl
                        bk = nb()
                        for k in range(4):
                            mm(bk, w[:, k, hl * 192:hl * 192 + 128], cqT[k], start=(k == 0), stop=(k == 3))
                        evac_copy(qnT[hh], bk)
                        ba = nb()
                        bb = nb()
                        for k in range(4):
                            mm(ba[0:64, :], w[:, k, hl * 192 + 128:hl * 192 + 192], cqT[k], start=(k == 0), stop=(k == 3))
                        for k in range(4):
                            mm(bb[0:64, :], w[:, k, 1152 + hl * 64:1152 + (hl + 1) * 64], cqT[k], start=(k == 0), stop=(k == 3))
                        rope_combine(qrT[hh], ba, bb)
                nblk_half = 4 * j + 4
                c.ring = [0, 1, 2]
                c.bank = 0
                for hh in range(12):
                    ob = banks[3 + 2 * (hh % 2)]
                    sbk = banks[4 + 2 * (hh % 2)]
                    first = True
                    chunks = []
                    for rk in range(2):
                        b0 = 0
                        while b0 < nblk_half:
                            nbk = min(4, nblk_half - b0)
                            chunks.append((rk, b0, nbk))
                            b0 += nbk
                    nblocks_total = 2 * nblk_half
                    pend = []
                    st_ = {"first": True, "done": 0}

                    def pv_stage(item, ob=ob, sbk=sbk, st_=st_, nblocks_total=nblocks_total):
                        vc_, bi, p, c0 = item
                        st_["done"] += 1
                        last = (st_["done"] == nblocks_total)
                        mm(ob[:, c0:TT], vc_[:, bi, :], p[:, c0:TT], start=st_["first"], stop=last, signal=True)
                        mm(sbk[:, c0:TT], ones, p[:, c0:TT], start=st_["first"], stop=last, signal=True)
                        st_["first"] = False

                    for (rk, b0, nbk) in chunks:
                        i_ = c.kvn
                        c.kvn = (c.kvn + 1) % NKV
                        kc_, vc_ = Kc[i_], Vc[i_]
                        key0 = rk * TOKC + b0 * 128
                        t_ = b0 // 4
                        P.dma("sp", kc_.ap[:, 0:TT], g1[t_][rk * 1600 + hh * 128:rk * 1600 + (hh + 1) * 128, :], reads=[g1dep[t_]],
                              writes=[kc_])
                        P.dma("sp", vc_.ap[:, 0:nbk, :],
                              g2v[t_][rk][:, hh * 128:(hh + 1) * 128].rearrange("(b p) d -> p b d", p=128),
                              reads=[g2dep[t_]], writes=[vc_])
                        for bi in range(nbk):
                            i = b0 + bi
                            if i < 4 * j:
                                c0 = 0
                                lp = None
                            else:
                                lp = i - 4 * j
                                c0 = lp * 128
                            kg0 = key0 + bi * 128
                            bk = nb()
                            mm(bk[:, c0:TT], kc_[:, bi * 128:(bi + 1) * 128], qnT[hh][:, c0:TT], start=True, stop=False, signal=False)
                            mm(bk[:, c0:TT], krTs[0:64, kg0:kg0 + 128], qrT[hh][0:64, c0:TT], start=False, stop=True)
                            p = ring(pT, "pTn")
                            act(p[:, c0:TT], bk[:, c0:TT], AF.Exp, scale=MLA_SCALE)
                            if lp is not None:
                                vtt(p[:, c0:c0 + 128], p[:, c0:c0 + 128], dmask[:, rk * 128:(rk + 1) * 128], ALU.mult)
                            pend.append((vc_, bi, p, c0))
                            if len(pend) > 2:
                                pv_stage(pend.pop(0))
                    while pend:
                        pv_stage(pend.pop(0))
                    rc = ring(tmpf, "tmpn")
                    vrecip(rc, sbk)
                    vtt(tokT[hh], ob, rc, ALU.mult)
                c.ring = [0, 1, 2, 3, 4, 5, 6]
                c.bank = 0
                out_proj_and_residual(b_w_out[0], tokT + memoT, G_NORM(1, 1))
                ffn(1)
                for sub in range(4):
                    for kq in range(4):
                        bk = nb()
                        for kk in range(4):
                            k = kq * 4 + kk
                            tr(bk[:, kk * 128:(kk + 1) * 128], h[k][:, sub * 128:(sub + 1) * 128], signal=(kk == 3))
                        evac_copy(xs[sub][:, kq * 512:(kq + 1) * 512], bk)
                    P.dma("sp", out[t0 + sub * 128:t0 + (sub + 1) * 128, :], xs[sub].ap, reads=[xs[sub]])


    except StopBuild:
        if mode == "A":
            def dump(tile, rb, cb, np_=128, ncol=512, conv=True):
                if conv:
                    t_ = ring(tmpf, "tmpn")
                    vcopy(t_[0:np_, 0:ncol], tile)
                    src = t_[0:np_, 0:ncol]
                else:
                    src = tile
                P.dma("sp", hT_d[rb * 128:rb * 128 + np_, cb * 512:cb * 512 + ncol], src.ap, reads=[src])
            sp_ = stop
            for k in range(NKC):
                if sp_ >= 4:
                    dump(h[k], k, 0, conv=False)
                if 5 <= sp_ < 10:
                    dump(hn[k], k, 1)
            for k in range(12):
                if 6 <= sp_ < 10:
                    dump(tokT[k], k, 2)
            for k in range(4):
                if 9 <= sp_ < 10:
                    dump(memoT[k], 12 + k, 2)
            dump(gv, 0, 3, ncol=256, conv=False)
            if sp_ >= 12:
                dump(Ctab, 5, 3, np_=64, conv=False)
                dump(Stab, 6, 3, np_=64, conv=False)
    fw = {"sp": [(P.sp_sems[i], P.sp_cnt[i]) for i in range(len(P.sp_sems)) if P.sp_cnt[i] > 0]}
    P.emit(fw)
    es.close()
    return nc, P


def _host_consts(r):
    ident = np.eye(128, dtype=np.float32)
    triu = np.triu(np.ones((128, 128), np.float32))
    ones_ = np.ones((128, 128), np.float32)
    zeros_ = np.zeros((128, 128), np.float32)
    def mk(rk):
        if rk < r:
            return ones_
        if rk == r:
            return triu
        return zeros_
    return np.concatenate([ident, triu, mk(0), mk(1)], axis=1)


def _shard_tokens(a, b, r):
    s = a.reshape(32, 128, *a.shape[1:])
    return np.ascontiguousarray(s[r::2].reshape(TOKC, *a.shape[1:]))


_NC_CACHE = {}


def _get_nc(mode):
    if mode not in _NC_CACHE:
        _NC_CACHE[mode] = build(mode)[0]
    return _NC_CACHE[mode]


def kernel(x, mem, positions, norm_gains, mem_norm, w_mem_kv, ffn_w_gate, ffn_w_up, ffn_w_down,
           a_w_in, a_ln_g, a_ln_b, a_w_s, a_b_s, a_w_out, kv_src_norm, w_kv_a, kv_norm, w_uk, w_uv,
           b_w_in, b_q_norm, b_w_uq, b_w_out):
    f = lambda a: np.ascontiguousarray(np.asarray(a, dtype=np.float32))
    x = f(x); mem = f(mem)
    positions = np.ascontiguousarray(np.asarray(positions, dtype=np.int32))
    vecs = np.zeros((256, 128), np.float32)
    vecs[0:128] = f(norm_gains).reshape(128, 128)
    vecs[128:144] = f(kv_src_norm).reshape(16, 128)
    vecs[144:176] = f(mem_norm).reshape(32, 128)
    vecs[176:180] = f(b_q_norm).reshape(4, 128)
    vecs[180:184] = f(kv_norm).reshape(4, 128)
    vecs[184:196] = f(a_ln_b).reshape(12, 128)
    vecs[196:208] = f(a_ln_g).reshape(12, 128)
    inv = (10000.0 ** (-np.arange(0, 64, 2, dtype=np.float32) / 64)).astype(np.float32)
    invf = np.concatenate([inv, inv]).reshape(64, 1).astype(np.float32)
    common = dict(vecs=vecs, invf=invf)
    w_mem_kv = f(w_mem_kv); ffn_w_gate = f(ffn_w_gate); ffn_w_up = f(ffn_w_up); ffn_w_down = f(ffn_w_down)
    lw = lambda l: {"w_mem_kv%d" % l: w_mem_kv[l], "ffn_w_gate%d" % l: ffn_w_gate[l], "ffn_w_up%d" % l: ffn_w_up[l],
                    "ffn_w_down%d" % l: ffn_w_down[l]}
    wA = dict(**lw(0), a_w_in=f(a_w_in), a_w_s=f(a_w_s), a_b_s=f(a_b_s), a_w_out=f(a_w_out), w_kv_a=f(w_kv_a), w_uk=f(w_uk), w_uv=f(w_uv))
    wB = dict(**lw(1), b_w_in=f(b_w_in), b_w_uq=f(b_w_uq), b_w_out=f(b_w_out))
    cores = list(range(8))
    percore = []
    for cid in cores:
        b, r = cid // 2, cid % 2
        percore.append(dict(cst=_host_consts(r), pos=_shard_tokens(positions[b], b, r).reshape(1, TOKC),
                            mem=np.ascontiguousarray(mem[b])))
    ncAB = _get_nc("AB")
    mapsAB = []
    for cid in cores:
        b, r = cid // 2, cid % 2
        m = dict(common); m.update(wA); m.update(wB); m.update(percore[cid])
        m["x"] = _shard_tokens(x[b], b, r)
        mapsAB.append(m)
    resB = run_bass_kernel_spmd(ncAB, mapsAB, core_ids=cores).results
    outp = np.empty((4, 32, 128, D), np.float32)
    for cid in cores:
        b, r = cid // 2, cid % 2
        outp[b, r::2] = resB[cid]["out"].reshape(16, 128, D)
    return outp.reshape(4, SEQ, D)
```

```python
import math
from contextlib import ExitStack
import numpy as np
import ml_dtypes
import concourse.bass as bass
import concourse.mybir as mybir
from concourse.bass_utils import run_bass_kernel_spmd

F32 = mybir.dt.float32
BF16 = mybir.dt.bfloat16
I32 = mybir.dt.int32
AF = mybir.ActivationFunctionType
ALU = mybir.AluOpType

D = 2048
NKC = 16
TT = 512
NTILE = 4
TOKC = 2048
SEQ = 4096
DFF = 5632
NFC = 44
EPS = 1e-6
MLA_SCALE = 192 ** -0.5
MEM_SCALE = 128 ** -0.5
TWO_PI = 2.0 * math.pi


class Buf:
    __slots__ = ("w", "r")

    def __init__(self):
        self.w = None
        self.r = {}


class Tl:
    __slots__ = ("ap", "bufs")

    def __init__(self, ap, bufs):
        self.ap = ap
        self.bufs = bufs

    def __getitem__(self, key):
        return Tl(self.ap[key], self.bufs)


class Prog:
    ENGS = ("pe", "act", "dve", "pool", "sp")

    def __init__(self, nc, es, n_sp_sems=8):
        self.nc = nc
        self.ins = {e: [] for e in self.ENGS}
        self.cnt = {e: 0 for e in self.ENGS}
        self.known = {e: {} for e in self.ENGS}
        self.esem = {e: es.enter_context(nc.semaphore("sem_" + e)) for e in self.ENGS}
        self.sp_sems = [es.enter_context(nc.semaphore("spd%d" % i)) for i in range(n_sp_sems)]
        self.sp_cnt = [0] * n_sp_sems
        self.sp_next = 0
        self.n_ins = 0
        act_sems = [es.enter_context(nc.semaphore("actd%d" % i)) for i in range(4)]
        self.rings = {"sp": [self.sp_sems, self.sp_cnt, 0], "act": [act_sems, [0] * 4, 0]}

    def custom(self, eng, fn, reads, writes, sem, amount):
        waits = self._deps(eng, reads, writes)
        self.ins[eng].append((waits, fn, (sem, amount)))
        self._mark((sem, amount), reads, writes)
        self.n_ins += 1

    def _deps(self, eng, reads, writes):
        waits = {}
        known = self.known[eng]
        pe_sem = self.esem["pe"]

        def need(sem, val):
            if eng == "pe" and sem is pe_sem:
                return
            if known.get(sem, 0) >= val:
                return
            if waits.get(sem, 0) < val:
                waits[sem] = val

        for t in reads:
            for b in t.bufs:
                if b.w is not None:
                    need(*b.w)
        for t in writes:
            for b in t.bufs:
                if b.w is not None:
                    need(*b.w)
                for s, v in b.r.items():
                    need(s, v)
        for s, v in waits.items():
            known[s] = v
        return list(waits.items())

    def _mark(self, tok, reads, writes):
        s, v = tok
        for t in reads:
            for b in t.bufs:
                if b.r.get(s, 0) < v:
                    b.r[s] = v
        for t in writes:
            for b in t.bufs:
                b.w = tok
                b.r = {}

    def op(self, eng, fn, reads=(), writes=(), signal=True):
        waits = self._deps(eng, reads, writes)
        if signal:
            self.cnt[eng] += 1
            tok = (self.esem[eng], self.cnt[eng])
        else:
            tok = (self.esem[eng], self.cnt[eng] + 1)
        self.ins[eng].append((waits, fn, (self.esem[eng], 1) if signal else None))
        self._mark(tok, reads, writes)
        self.n_ins += 1
        return tok

    def dma(self, eng, out_ap, in_ap, reads=(), writes=(), sem=None, semcnt=None, chain=False):
        if sem is None:
            rg = self.rings[eng]
            j = rg[2]
            rg[2] = (j + 1) % len(rg[0])
            sem = rg[0][j]
            prev = rg[1][j]
            rg[1][j] += 16
            val = rg[1][j]
        else:
            prev = semcnt[0]
            semcnt[0] += 16
            val = semcnt[0]
        waits = dict(self._deps(eng, reads, writes))
        if (not chain) and prev > 0 and self.known[eng].get(sem, 0) < prev:
            waits[sem] = max(waits.get(sem, 0), prev)
            self.known[eng][sem] = prev
        self.ins[eng].append((list(waits.items()),
                              lambda e, o=out_ap, i=in_ap: e.dma_start(out=o, in_=i), (sem, 16)))
        tok = (sem, val)
        self._mark(tok, reads, writes)
        self.n_ins += 1
        return tok

    def emit(self, final_waits):
        nc = self.nc
        prog = self

        def run(eng_name, e):
            for waits, fn, inc in prog.ins[eng_name]:
                for s, v in waits:
                    e.wait_ge(s, v)
                ins = fn(e)
                if inc is not None:
                    ins.then_inc(inc[0], inc[1])
            for s, v in final_waits.get(eng_name, []):
                e.wait_ge(s, v)

        with nc.Block() as block:
            @block.tensor
            def _(e):
                run("pe", e)

            @block.scalar
            def _(e):
                run("act", e)

            @block.vector
            def _(e):
                run("dve", e)

            @block.gpsimd
            def _(e):
                run("pool", e)

            @block.sync
            def _(e):
                run("sp", e)


class Ctx:
    pass


class StopBuild(Exception):
    pass


def build(mode, stop=None):
    nc = bass.Bass("TRN2", target_bir_lowering=False)
    es = ExitStack()
    P = Prog(nc, es)
    c = Ctx()

    def stage(n):
        if stop is not None and n > stop:
            raise StopBuild()

    doA = mode in ("A", "AB")
    doB = mode in ("B", "AB")

    def din(name, shape, dt=F32):
        return nc.dram_tensor(name, list(shape), dt, kind="ExternalInput").ap()

    def dout(name, shape, dt=F32):
        return nc.dram_tensor(name, list(shape), dt, kind="ExternalOutput").ap()

    def dint(name, shape, dt=F32):
        return nc.dram_tensor(name, list(shape), dt, kind="Internal").ap()

    vecs = din("vecs", [256, 128])
    cst = din("cst", [128, 512])
    invf = din("invf", [64, 1])
    pos = din("pos", [1, TOKC], I32)
    mem = din("mem", [256, D])
    layers = ([0] if doA else []) + ([1] if doB else [])
    w_mem_kv = {l: din("w_mem_kv%d" % l, [D, 1024]) for l in layers}
    ffn_w_gate = {l: din("ffn_w_gate%d" % l, [D, DFF]) for l in layers}
    ffn_w_up = {l: din("ffn_w_up%d" % l, [D, DFF]) for l in layers}
    ffn_w_down = {l: din("ffn_w_down%d" % l, [DFF, D]) for l in layers}
    if doA:
        x = din("x", [TOKC, D])
        a_w_in = din("a_w_in", [1, D, 3584])
        a_w_s = din("a_w_s", [1, 12, 128, 128])
        a_b_s = din("a_b_s", [1, 12, 128])
        a_w_out = din("a_w_out", [1, D, D])
        w_kv_a = din("w_kv_a", [D, 576])
        w_uk = din("w_uk", [512, 1536])
        w_uv = din("w_uv", [512, 1536])
    if doB:
        b_w_in = din("b_w_in", [1, D, 1024])
        b_w_uq = din("b_w_uq", [1, 512, 2304])
        b_w_out = din("b_w_out", [1, D, D])
        out = dout("out", [TOKC, D])
    if mode == "A":
        hT_d = dout("hT", [D, TOKC])
        KT_loc = dout("KT", [12, 128, TOKC], BF16)
        krT_loc = dout("krT", [64, TOKC], BF16)
        V_loc = dout("V", [TOKC, 1536], BF16)
    elif mode == "B":
        hT_d = din("hT", [D, TOKC])
        KT_full = din("KT", [12, 128, SEQ], BF16)
        krT_full = din("krT", [64, SEQ], BF16)
        V_full = din("V", [SEQ, 1536], BF16)
    else:
        hT_d = dint("hT_scr", [D, TOKC])
        l1 = [dint("l1_%d" % t, [1600, TT], BF16) for t in range(NTILE)]
        l2 = [dint("l2_%d" % t, [1536, TT], BF16) for t in range(NTILE)]
        g1 = [dint("g1_%d" % t, [2 * 1600, TT], BF16) for t in range(NTILE)]
        g2 = [dint("g2_%d" % t, [2 * 1536, TT], BF16) for t in range(NTILE)]
        l2v = [a_.rearrange("r c -> (r c)").rearrange("(t d) -> t d", d=1536) for a_ in l2]
        g2v = [[g2[t][rk * 1536:(rk + 1) * 1536, :].rearrange("r c -> (r c)").rearrange("(t d) -> t d", d=1536)
                for rk in range(2)] for t in range(NTILE)]
        ccsems = [es.enter_context(nc.semaphore("ccs%d" % i)) for i in range(2 * NTILE)]
        g1dep = [Tl(None, [Buf()]) for _ in range(NTILE)]
        g2dep = [Tl(None, [Buf()]) for _ in range(NTILE)]
    if mode == "B":
        KTg = [KT_full[:, :, rk * TOKC:(rk + 1) * TOKC] for rk in range(2)]
        krTg = [krT_full[:, rk * TOKC:(rk + 1) * TOKC] for rk in range(2)]
        Vg = [V_full[rk * TOKC:(rk + 1) * TOKC, :] for rk in range(2)]
    kvst_bufs = []
    kvg_dep = Tl(None, [Buf()])
    hdep = [[Tl(None, [Buf()]) for _ in range(NKC)] for _ in range(NTILE)]

    def sb(name, shape, dt):
        return es.enter_context(nc.sbuf_tensor("sb_" + name, list(shape), dt))

    def gran(n):
        return [Buf() for _ in range(n)]

    hA = sb("hA", [128, NKC, TT], F32)
    h = [Tl(hA[:, k, :], [Buf()]) for k in range(NKC)]

    regB = sb("regB", [128, 8192], F32)
    gB = gran(32)
    regC = sb("regC", [128, 11264], F32)
    gC = gran(45)

    def view(reg, grans, f0, f1, dt, inner):
        ap = reg[:, f0:f1]
        if dt == BF16:
            ap = ap.bitcast(BF16)
            nel = (f1 - f0) * 2
            bpe = 2
        else:
            nel = f1 - f0
            bpe = 4
        n = nel // inner
        ap3 = ap.rearrange("p (k n) -> p k n", n=inner)
        tiles = []
        for k in range(n):
            b0 = (f0 * 4 + k * inner * bpe) // 1024
            b1 = (f0 * 4 + (k + 1) * inner * bpe - 1) // 1024
            tiles.append(Tl(ap3[:, k, :], grans[b0:b1 + 1]))
        return tiles

    hn = view(regB, gB, 0, 4096, BF16, TT)
    tokT = view(regB, gB, 4096, 7168, BF16, TT)
    memoT = view(regB, gB, 7168, 8192, BF16, TT)
    fT = view(regB, gB, 0, 8192, F32, TT)
    vf = view(regC, gC, 0, 6144, F32, 1536)
    vn = view(regC, gC, 6144, 9216, BF16, 1536)
    qmT = view(regC, gC, 9216, 10240, BF16, TT)
    mixT = view(regC, gC, 0, 8192, F32, TT)
    actT = view(regC, gC, 0, 11264, BF16, TT)
    xs = view(regC, gC, 0, 8192, F32, D)
    cqT = view(regC, gC, 8192, 9216, BF16, TT)
    qnT = view(regC, gC, 0, 3072, BF16, TT)
    qrT = view(regC, gC, 3072, 6144, BF16, TT)
    cqf = view(regC, gC, 6144, 8192, F32, TT)
    ckf = view(regC, gC, 0, 2048, F32, TT)
    ckT = view(regC, gC, 2048, 3072, BF16, TT)
    kst = view(regC, gC, 3072, 4096, BF16, TT)
    vst = view(regC, gC, 4096, 5632, BF16, 1536)
    krs = view(regC, gC, 5632, 6144, BF16, TT)
    memst = view(regC, gC, 0, 4096, F32, D)
    memTf = view(regC, gC, 4096, 6144, F32, 256)
    memTf2 = view(regC, gC, 6144, 8192, F32, 256)
    memTf = memTf + memTf2
    memnT = view(regC, gC, 8192, 10240, BF16, 256)

    NS = 3
    wslot_t = sb("wslots", [128, NS, 8192], BF16)
    wslots = [Tl(wslot_t[:, s, :], [Buf()]) for s in range(NS)]
    wsem = [es.enter_context(nc.semaphore("wsem%d" % s)) for s in range(NS)]
    wcnt = [[0] for _ in range(NS)]
    c.wnext = 0
    c.cc_pending = []
    cc_sem = es.enter_context(nc.semaphore("cc_sem"))

    cstf_t = sb("cstf", [128, 256], F32)
    cstf = Tl(cstf_t[:, :], [Buf()])
    ident = cstf[:, 0:128]
    triu = cstf[:, 128:256]
    dmask_t = sb("dmask", [128, 256], BF16)
    dmask = Tl(dmask_t[:, :], [Buf()])
    ones_t = sb("ones", [128, 128], BF16)
    ones = Tl(ones_t[:, :], [Buf()])
    gv_t = sb("gv", [128, 256], F32)
    gv = Tl(gv_t[:, :], [Buf()])
    invf_t = sb("invf", [64, 1], F32)
    invfT = Tl(invf_t[:, :], [Buf()])
    sq_t = sb("sq", [128, 4, TT], BF16)
    sq = [Tl(sq_t[:, i, :], [Buf()]) for i in range(4)]
    c.sqn = 0
    tmp_t = sb("tmpf", [128, 4, TT], F32)
    tmpf = [Tl(tmp_t[:, i, :], [Buf()]) for i in range(4)]
    c.tmpn = 0
    rstd_t = sb("rstd", [128, 2, TT], F32)
    rstd = [Tl(rstd_t[:, i, :], [Buf()]) for i in range(2)]
    c.rstdn = 0
    pT_t = sb("pT", [128, 4, TT], BF16)
    pT = [Tl(pT_t[:, i, :], [Buf()]) for i in range(4)]
    c.pTn = 0
    KmT_t = sb("KmT", [128, 4, 256], BF16)
    KmT = Tl(KmT_t[:, :, :], [Buf()])
    Vm_t = sb("Vm", [128, 2, 512], BF16)
    Vm = Tl(Vm_t[:, :, :], [Buf()])
    tab_t = sb("tabs", [64, 2, TT], F32)
    Ctab = Tl(tab_t[:, 0, :], [Buf()])
    Stab = Tl(tab_t[:, 1, :], [Buf()])
    st_t = sb("stats", [128, 4, 32], F32)
    stt = [Tl(st_t[:, i, :], [Buf()]) for i in range(4)]
    regD = sb("regD", [128, 5120], F32)
    gD = gran(20)
    if doA:
        wsT = Tl(regD[:, 0:768].bitcast(BF16).rearrange("p (g t) -> p g t", t=128), gD[0:3])
        Bt = Tl(regD[:, 768:2304].rearrange("p (g t) -> p g t", t=128), gD[3:9])
        bsb = view(regC, gC, 8192, 9728, F32, 1536)[0]
    if doB:
        krTs = Tl(regD[:, 0:2048].bitcast(BF16), gD[0:8])
        NKV = 6
        Kc = [Tl(regD[:, 2048 + i * 256:2048 + (i + 1) * 256].bitcast(BF16), gD[8 + i:9 + i]) for i in range(NKV)]
        Vc = [Tl(regD[:, 3584 + i * 256:3584 + (i + 1) * 256].bitcast(BF16).rearrange("p (b d) -> p b d", d=128),
                 gD[14 + i:15 + i]) for i in range(NKV)]
        c.kvn = 0

    banks = []
    for i in range(8):
        t = es.enter_context(nc.psum_tensor("ps%d" % i, [128, 512], F32))
        banks.append(Tl(t[:, :], [Buf()]))
    c.bank = 0
    c.ring = [0, 1, 2, 3, 4, 5, 6]
    STAT = banks[7]

    def nb():
        c.bank = (c.bank + 1) % len(c.ring)
        return banks[c.ring[c.bank]]

    def ring(lst, attr):
        i = getattr(c, attr)
        setattr(c, attr, (i + 1) % len(lst))
        return lst[i]

    def mm(out, lhsT, rhs, start, stop, signal=None):
        if signal is None:
            signal = stop
        P.op("pe", lambda e, o=out.ap, l=lhsT.ap, r=rhs.ap, s=start, t=stop: e.matmul(o, l, r, start=s, stop=t),
             reads=[lhsT, rhs], writes=[out], signal=signal)

    def tr(out, in_, signal=True):
        P.op("pe", lambda e, o=out.ap, i=in_.ap, d=ident.ap: e.transpose(o, i, d),
             reads=[in_, ident], writes=[out], signal=signal)

    def act(out, in_, func, scale=1.0, reads=(), bias=None):
        if bias is None:
            P.op("act", lambda e, o=out.ap, i=in_.ap, f=func, s=scale: e.activation(out=o, in_=i, func=f, scale=s),
                 reads=[in_] + list(reads), writes=[out])
        else:
            P.op("act", lambda e, o=out.ap, i=in_.ap, f=func, s=scale, b=bias: e.activation(out=o, in_=i, func=f, scale=s, bias=b),
                 reads=[in_] + list(reads), writes=[out])

    def vts(out, in0, s1, s2, op0, op1, reads=()):
        if s2 is None:
            P.op("dve", lambda e, o=out.ap, i=in0.ap: e.tensor_scalar(o, i, s1, None, op0),
                 reads=[in0] + list(reads), writes=[out])
        else:
            P.op("dve", lambda e, o=out.ap, i=in0.ap: e.tensor_scalar(o, i, s1, s2, op0, op1),
                 reads=[in0] + list(reads), writes=[out])

    def vstt(out, in0, scalar, in1, op0, op1, reads=()):
        P.op("dve", lambda e, o=out.ap, i0=in0.ap, i1=in1.ap: e.scalar_tensor_tensor(o, i0, scalar, i1, op0, op1),
             reads=[in0, in1] + list(reads), writes=[out])

    def vtt(out, in0, in1, op):
        P.op("dve", lambda e, o=out.ap, i0=in0.ap, i1=in1.ap: e.tensor_tensor(o, i0, i1, op),
             reads=[in0, in1], writes=[out])

    def vcopy(out, in_):
        P.op("dve", lambda e, o=out.ap, i=in_.ap: e.tensor_copy(o, i), reads=[in_], writes=[out])

    def vrecip(out, in_):
        P.op("dve", lambda e, o=out.ap, i=in_.ap: e.reciprocal(o, i), reads=[in_], writes=[out])

    c.cp = 0

    def evac_copy(out, in_):
        c.cp ^= 1
        if c.cp:
            act(out, in_, AF.Copy)
        else:
            vcopy(out, in_)

    def wload(parts):
        s = c.wnext
        c.wnext = (s + 1) % NS
        slot = wslots[s]
        for pi, (dst_ap, src_ap) in enumerate([(p[0](wslot_t, s), p[1]) for p in parts]):
            P.dma("pool", dst_ap, src_ap, reads=(), writes=[slot] if pi == 0 else [], sem=wsem[s], semcnt=wcnt[s],
                  chain=(pi > 0))
        slot.bufs[0].w = (wsem[s], wcnt[s][0])
        return s, slot

    def wview(s, nk, ncol):
        return wslot_t[:, s, 0:nk * ncol].rearrange("p (k n) -> p k n", n=ncol)

    def wload_std(wmat, k0, nk, c0, ncol):
        src = wmat[k0 * 128:(k0 + nk) * 128, c0:c0 + ncol].rearrange("(k p) n -> p k n", p=128)
        s, slot = wload([(lambda t, s_, nk=nk, ncol=ncol: t[:, s_, 0:nk * ncol].rearrange("p (k n) -> p k n", n=ncol), src)])
        return Tl(wview(s, nk, ncol), slot.bufs)

    gcol = lambda j: gv_t[:, j:j + 1]

    def stat_accum(stat_bank, src, first, last):
        s = ring(sq, "sqn")
        act(s, src, AF.Square)
        mm(stat_bank, ones, s, start=first, stop=last, signal=True)

    def make_rstd(stat_bank, dim):
        r = ring(rstd, "rstdn")
        act(r, stat_bank, AF.Sqrt, scale=1.0 / dim, bias=float(EPS))
        vrecip(r, r)
        return r

    def mm4_kmajor(w, rhs):
        bks = [nb() for _ in range(4)]
        nk = len(rhs)
        for k in range(nk):
            for mi in range(4):
                mm(bks[mi], w[:, k, mi * 128:(mi + 1) * 128], rhs[k], start=(k == 0), stop=(k == nk - 1),
                   signal=(k == nk - 1))
        return bks

    def rms_to(srcs, dsts, gj, dim, split=False):
        sbk = STAT
        n = len(srcs)
        for k in range(n):
            if split and (k % 2 == 1):
                s_ = ring(sq, "sqn")
                vtt(s_, srcs[k], srcs[k], ALU.mult)
                mm(sbk, ones, s_, start=(k == 0), stop=(k == n - 1), signal=True)
            else:
                stat_accum(sbk, srcs[k], k == 0, k == n - 1)
        r = make_rstd(sbk, dim)
        for k in range(n):
            vstt(dsts[k], srcs[k], gcol(gj + k), r, ALU.mult, ALU.mult, reads=[gv])

    def residual_add(srcs, r, gj):
        for k in range(NKC):
            vstt(srcs[k], srcs[k], gcol(gj + k), r, ALU.mult, ALU.mult, reads=[gv])
            vtt(h[k], h[k], srcs[k], ALU.add)

    try:
        P.dma("sp", cstf.ap, cst[:, 0:256], writes=[cstf])
        P.dma("pool", dmask.ap, cst[:, 256:512], writes=[dmask], sem=wsem[0], semcnt=wcnt[0])
        P.dma("sp", invfT.ap, invf[:, :], writes=[invfT])
        P.op("dve", lambda e: e.memset(ones_t[:, :], 1.0), writes=[ones])
        for half in range(2):
            st = xs[half][:, 0:128]
            P.dma("sp", st.ap, vecs[half * 128:(half + 1) * 128, :], writes=[st])
            bk = nb()
            tr(bk[:, 0:128], st)
            vcopy(gv[:, half * 128:(half + 1) * 128], bk[:, 0:128])
        stage(1)
        G_NORM = lambda l, n: l * 64 + n * 16
        G_KVSRC = 128
        G_MEM = lambda l: 144 + l * 16
        G_BQ = 176
        G_KVN = 180
        G_LNB = 184
        G_LNG = 196

        def setup_mem(l):
            for t in range(2):
                P.dma("sp", memst[t].ap, mem[t * 128:(t + 1) * 128, :], writes=[memst[t]])
            for k in range(NKC):
                bk = nb()
                for t in range(2):
                    tr(bk[:, t * 128:(t + 1) * 128], memst[t][:, k * 128:(k + 1) * 128], signal=(t == 1))
                evac_copy(memTf[k], bk[:, 0:256])
            sbk = STAT
            for k in range(NKC):
                s = ring(sq, "sqn")
                act(s[:, 0:256], memTf[k], AF.Square)
                mm(sbk[:, 0:256], ones, s[:, 0:256], start=(k == 0), stop=(k == NKC - 1), signal=True)
            r = ring(rstd, "rstdn")
            act(r[:, 0:256], sbk[:, 0:256], AF.Sqrt, scale=1.0 / D, bias=float(EPS))
            vrecip(r[:, 0:256], r[:, 0:256])
            for k in range(NKC):
                vstt(memnT[k], memTf[k], gcol(G_MEM(l) + k), r[:, 0:256], ALU.mult, ALU.mult, reads=[gv])
            wk = wload_std(w_mem_kv[l], 0, 16, 0, 512)
            for hh in range(4):
                bk = nb()
                for k in range(NKC):
                    mm(bk[:, 0:256], wk[:, k, hh * 128:(hh + 1) * 128], memnT[k], start=(k == 0), stop=(k == NKC - 1))
                evac_copy(KmT[:, hh, :], bk[:, 0:256])
            wv = wload_std(w_mem_kv[l], 0, 16, 512, 512)
            for mt in range(2):
                bk = nb()
                for k in range(NKC):
                    mm(bk, memnT[k][:, mt * 128:(mt + 1) * 128], wv[:, k, :], start=(k == 0), stop=(k == NKC - 1))
                evac_copy(Vm[:, mt, :], bk)

        def mem_attention():
            def pv(hh, pts):
                ob = nb()
                sbk = nb()
                for mt in range(2):
                    mm(ob, Vm[:, mt, hh * 128:(hh + 1) * 128], pts[mt], start=(mt == 0), stop=(mt == 1))
                for mt in range(2):
                    mm(sbk, ones, pts[mt], start=(mt == 0), stop=(mt == 1))
                rc = ring(tmpf, "tmpn")
                vrecip(rc, sbk)
                vtt(memoT[hh], ob, rc, ALU.mult)

            prev = None
            for hh in range(4):
                pts = []
                for mt in range(2):
                    bk = nb()
                    mm(bk, KmT[:, hh, mt * 128:(mt + 1) * 128], qmT[hh], start=True, stop=True)
                    p = ring(pT, "pTn")
                    act(p, bk, AF.Exp, scale=MEM_SCALE)
                    pts.append(p)
                if prev is not None:
                    pv(*prev)
                prev = (hh, pts)
            pv(*prev)

        def tables(t0):
            pt_ = ring(tmpf, "tmpn")
            posi = Tl(pt_.ap[0:64, :].bitcast(I32), pt_.bufs)
            P.dma("sp", posi.ap, pos[0, t0:t0 + TT].partition_broadcast(64), writes=[posi])
            a0 = ring(tmpf, "tmpn")[0:64, :]
            a1 = ring(tmpf, "tmpn")[0:64, :]
            C1 = 6.28125
            C2 = TWO_PI - 6.28125
            vcopy(a0, posi)
            vts(a0, a0, invf_t[:, 0:1], None, ALU.mult, None, reads=[invfT])
            vts(a1, a0, float(1.0 / TWO_PI), None, ALU.mult, None)
            ki = Tl(posi.ap, posi.bufs)
            vcopy(ki, a1)
            vcopy(a1, ki)
            vstt(a0, a1, float(-C1), a0, ALU.mult, ALU.add)
            vstt(a0, a1, float(-C2), a0, ALU.mult, ALU.add)
            vts(a1, a0, float(math.pi / 2), float(-TWO_PI), ALU.is_gt, ALU.mult)
            vstt(a1, a0, float(math.pi / 2), a1, ALU.add, ALU.add)
            PI_ = 3.1415925
            vts(a1, a1, PI_, -PI_, ALU.min, ALU.max)
            vts(a0, a0, PI_, -PI_, ALU.min, ALU.max)
            act(Ctab, a1, AF.Sin, scale=1.0)
            act(Stab[0:32, :], a0[0:32, :], AF.Sin, scale=-1.0)
            act(Stab[32:64, :], a0[32:64, :], AF.Sin, scale=1.0)

        def rope_combine(dst, bk_a, bk_b):
            t1 = ring(tmpf, "tmpn")
            t2 = ring(tmpf, "tmpn")
            vtt(t1[0:64, :], bk_a[0:64, :], Ctab, ALU.mult)
            vtt(t2[0:64, :], bk_b[0:64, :], Stab, ALU.mult)
            vtt(dst[0:64, :], t1[0:64, :], t2[0:64, :], ALU.add)

        def out_proj_and_residual(wmat, rhs_list, gj):
            sbk = STAT
            pend_ = None
            for blk in range(4):
                w = wload_std(wmat, 0, 16, blk * 512, 512)
                for mi in range(4):
                    m = blk * 4 + mi
                    bk = nb()
                    for k in range(NKC):
                        mm(bk, w[:, k, mi * 128:(mi + 1) * 128], rhs_list[k], start=(k == 0), stop=(k == NKC - 1))
                    act(mixT[m], bk, AF.Copy)
                    s_ = ring(sq, "sqn")
                    act(s_, mixT[m], AF.Square)
                    if pend_ is not None:
                        mm(sbk, ones, pend_[1], start=(pend_[0] == 0), stop=False, signal=True)
                    pend_ = (m, s_)
            mm(sbk, ones, pend_[1], start=False, stop=True, signal=True)
            r = make_rstd(sbk, D)
            residual_add(mixT, r, gj)

        def ffn(l):
            rms_to(h, hn, G_NORM(l, 2), D)
            wg_m, wu_m, wd_m = ffn_w_gate[l], ffn_w_up[l], ffn_w_down[l]
            for blk in range(11):
                wg = wload_std(wg_m, 0, 16, blk * 512, 512)
                wu = wload_std(wu_m, 0, 16, blk * 512, 512)
                sgs = []
                bgs_ = mm4_kmajor(wg, hn) if blk == 0 else None
                for mi in range(4):
                    if bgs_ is not None:
                        bg = bgs_[mi]
                    else:
                        bg = nb()
                        for k in range(NKC):
                            mm(bg, wg[:, k, mi * 128:(mi + 1) * 128], hn[k], start=(k == 0), stop=(k == NKC - 1))
                    sg = ring(tmpf, "tmpn")
                    act(sg, bg, AF.Silu)
                    sgs.append(sg)
                for mi in range(4):
                    m = blk * 4 + mi
                    bu = nb()
                    for k in range(NKC):
                        mm(bu, wu[:, k, mi * 128:(mi + 1) * 128], hn[k], start=(k == 0), stop=(k == NKC - 1))
                    vtt(actT[m], sgs[mi], bu, ALU.mult)
            sbk = STAT
            dpend = []
            for blk in range(4):
                bks = [nb() for _ in range(4)]
                for kg in range(3):
                    nk = 16 if kg < 2 else 12
                    w = wload_std(wd_m, kg * 16, nk, blk * 512, 512)
                    for kl in range(nk):
                        k = kg * 16 + kl
                        for mi in range(4):
                            last_of_slot = (kl == nk - 1 and mi == 3)
                            stop = (k == NFC - 1)
                            mm(bks[mi], w[:, kl, mi * 128:(mi + 1) * 128], actT[k], start=(k == 0), stop=stop,
                               signal=(stop or last_of_slot))
                    if kg == 0:
                        for (m_, s_) in dpend:
                            mm(sbk, ones, s_, start=(m_ == 0), stop=False, signal=True)
                        dpend = []
                for mi in range(4):
                    m = blk * 4 + mi
                    act(fT[m], bks[mi], AF.Copy)
                    s_ = ring(sq, "sqn")
                    act(s_, fT[m], AF.Square)
                    dpend.append((m, s_))
            for (m_, s_) in dpend:
                mm(sbk, ones, s_, start=False, stop=(m_ == NKC - 1), signal=True)
            r = make_rstd(sbk, D)
            residual_add(fT, r, G_NORM(l, 3))

        if doA:
            for g in range(12):
                st = xs[0][:, g * 128:(g + 1) * 128]
                P.dma("sp", st.ap, a_w_s[0, g, :, :], writes=[st])
            P.dma("sp", bsb.ap, a_b_s[0, :, :].rearrange("g t -> (g t)").partition_broadcast(128), writes=[bsb])
            for g in range(12):
                bk = nb()
                tr(bk[:, 0:128], xs[0][:, g * 128:(g + 1) * 128])
                vtt(wsT[:, g, :], bk[:, 0:128], triu, ALU.mult)
            for g in range(12):
                bk = nb()
                mm(bk[:, 0:128], ones, wsT[:, g, :], start=True, stop=True)
                vstt(Bt[:, g, :], bk[:, 0:128], gcol(G_LNB + g), bsb[:, g * 128:(g + 1) * 128], ALU.mult, ALU.add, reads=[gv])
            stage(2)
            setup_mem(0)
            stage(3)

            for tt in range(NTILE):
                t0 = tt * TT
                for sub in range(4):
                    P.dma("act", xs[sub].ap, x[t0 + sub * 128:t0 + (sub + 1) * 128, :], writes=[xs[sub]])
                for sub in range(4):
                    for kq in range(4):
                        bk = nb()
                        for kk in range(4):
                            k = kq * 4 + kk
                            tr(bk[:, kk * 128:(kk + 1) * 128], xs[sub][:, k * 128:(k + 1) * 128], signal=(kk == 3))
                        o_tl = Tl(hA[:, kq * 4:(kq + 1) * 4, sub * 128:(sub + 1) * 128],
                                  [h[kq * 4 + kk].bufs[0] for kk in range(4)])
                        i_tl = Tl(bk.ap.rearrange("p (a b) -> p a b", b=128), bk.bufs)
                        evac_copy(o_tl, i_tl)
                stage(4 if tt == 0 else 13)
                rms_to(h, hn, G_NORM(0, 0), D, split=True)
                tables(t0)
                stage(5 if tt == 0 else 13)
                win = a_w_in[0]
                for blk in range(3):
                    w = wload_std(win, 0, 16, 1536 + blk * 512, 512)
                    if blk == 0:
                        bks_ = [nb() for _ in range(4)]
                        for k in range(NKC):
                            for sub in range(4):
                                mm(bks_[sub], hn[k][:, sub * 128:(sub + 1) * 128], w[:, k, :], start=(k == 0), stop=(k == NKC - 1),
                                   signal=(k == NKC - 1))
                        for sub in range(4):
                            act(vf[sub][:, 0:512], bks_[sub], AF.Gelu_apprx_tanh)
                        continue
                    for sub in range(4):
                        bk = nb()
                        for k in range(NKC):
                            mm(bk, hn[k][:, sub * 128:(sub + 1) * 128], w[:, k, :], start=(k == 0), stop=(k == NKC - 1))
                        act(vf[sub][:, blk * 512:(blk + 1) * 512], bk, AF.Gelu_apprx_tanh)
                for sub in range(4):
                    s6 = stt[sub]
                    for j in range(3):
                        P.op("dve", lambda e, o=st_t[:, sub, j * 6:(j + 1) * 6], i=vf[sub].ap[:, j * 512:(j + 1) * 512]: e.bn_stats(o, i),
                             reads=[vf[sub]], writes=[s6])
                    P.op("dve", lambda e, o=st_t[:, sub, 24:26], i=st_t[:, sub, 0:18]: e.bn_aggr(o, i), reads=[s6], writes=[s6])
                    P.op("act", lambda e, o=st_t[:, sub, 26:27], i=st_t[:, sub, 25:26]: e.activation(out=o, in_=i, func=AF.Sqrt, scale=1.0, bias=float(EPS)),
                         reads=[s6], writes=[s6])
                    P.op("dve", lambda e, o=st_t[:, sub, 26:27], i=st_t[:, sub, 26:27]: e.reciprocal(o, i), reads=[s6], writes=[s6])
                    P.op("dve", lambda e, o=vn[sub].ap, i=vf[sub].ap, m_=st_t[:, sub, 24:25], r_=st_t[:, sub, 26:27]:
                         e.tensor_scalar(o, i, m_, r_, ALU.subtract, ALU.mult), reads=[vf[sub], s6], writes=[vn[sub]])
                for blk in range(3):
                    w = wload_std(win, 0, 16, blk * 512, 512)
                    for mi in range(4):
                        bk = nb()
                        for k in range(NKC):
                            mm(bk, w[:, k, mi * 128:(mi + 1) * 128], hn[k], start=(k == 0), stop=(k == NKC - 1))
                        act(tokT[blk * 4 + mi], bk, AF.Gelu_apprx_tanh)
                w = wload_std(win, 0, 16, 3072, 512)
                for mi in range(4):
                    bk = nb()
                    for k in range(NKC):
                        mm(bk, w[:, k, mi * 128:(mi + 1) * 128], hn[k], start=(k == 0), stop=(k == NKC - 1))
                    evac_copy(qmT[mi], bk)
                while c.cc_pending:
                    c.cc_pending.pop(0)()
                stage(6 if tt == 0 else 13)
                stage(7 if tt == 0 else 13)
                for g in range(12):
                    bk = nb()
                    for sub in range(4):
                        mm(bk[:, sub * 128:(sub + 1) * 128], vn[sub][:, g * 128:(g + 1) * 128], wsT[:, g, :], start=True, stop=True,
                           signal=(sub == 3))
                    t1 = ring(tmpf, "tmpn")
                    for sub in range(4):
                        vstt(t1[:, sub * 128:(sub + 1) * 128], bk[:, sub * 128:(sub + 1) * 128], gcol(G_LNG + g), Bt[:, g, :],
                             ALU.mult, ALU.add, reads=[gv])
                    vtt(tokT[g], t1, tokT[g], ALU.mult)
                stage(8 if tt == 0 else 13)
                mem_attention()
                stage(9 if tt == 0 else 13)
                out_proj_and_residual(a_w_out[0], tokT + memoT, G_NORM(0, 1))
                stage(10 if tt == 0 else 13)
                ffn(0)
                stage(11 if tt == 0 else 13)
                rms_to(h, hn, G_KVSRC, D)
                w = wload_std(w_kv_a, 0, 16, 0, 512)
                sbk = STAT
                bks_ = mm4_kmajor(w, hn)
                for mi in range(4):
                    bk = bks_[mi]
                    act(ckf[mi], bk, AF.Copy)
                    stat_accum(sbk, ckf[mi], mi == 0, mi == 3)
                r = make_rstd(sbk, 512)
                for mi in range(4):
                    vstt(ckT[mi], ckf[mi], gcol(G_KVN + mi), r, ALU.mult, ALU.mult, reads=[gv])
                srcs = [
                    (lambda t, s_: t[:, s_, 0:16 * 128].rearrange("p (k n) -> p k n", n=128)[:, :, 0:64],
                     w_kv_a[:, 512:576].rearrange("(k p) n -> p k n", p=128)),
                    (lambda t, s_: t[:, s_, 0:16 * 128].rearrange("p (k n) -> p k n", n=128)[:, :, 64:96],
                     w_kv_a[:, 544:576].rearrange("(k p) n -> p k n", p=128)),
                    (lambda t, s_: t[:, s_, 0:16 * 128].rearrange("p (k n) -> p k n", n=128)[:, :, 96:128],
                     w_kv_a[:, 512:544].rearrange("(k p) n -> p k n", p=128)),
                ]
                s_, slot = wload(srcs)
                wr = Tl(wview(s_, 16, 128), slot.bufs)
                ba = nb()
                bb = nb()
                for k in range(NKC):
                    mm(ba[0:64, :], wr[:, k, 0:64], hn[k], start=(k == 0), stop=(k == NKC - 1))
                for k in range(NKC):
                    mm(bb[0:64, :], wr[:, k, 64:128], hn[k], start=(k == 0), stop=(k == NKC - 1))
                kro = krs[tt % 2]
                rope_combine(kro, ba, bb)
                st1, st2 = [], []
                st1.append(Tl(None, [Buf()]))
                P.dma("sp", l1[tt][1536:1600, :], kro.ap[0:64, :], reads=[kro], writes=[st1[-1]])
                for blk in range(3):
                    w = wload_std(w_uk, 0, 4, blk * 512, 512)
                    for mi in range(4):
                        hh = blk * 4 + mi
                        bk = nb()
                        for k in range(4):
                            mm(bk, w[:, k, mi * 128:(mi + 1) * 128], ckT[k], start=(k == 0), stop=(k == 3))
                        ks = kst[hh % 4]
                        evac_copy(ks, bk)
                        st1.append(Tl(None, [Buf()]))
                        P.dma("sp", l1[tt][hh * 128:(hh + 1) * 128, :], ks.ap, reads=[ks], writes=[st1[-1]])
                for blk in range(3):
                    w = wload_std(w_uv, 0, 4, blk * 512, 512)
                    for sub in range(4):
                        bk = nb()
                        for k in range(4):
                            mm(bk, ckT[k][:, sub * 128:(sub + 1) * 128], w[:, k, :], start=(k == 0), stop=(k == 3))
                        vs = kst[sub]
                        evac_copy(vs, bk)
                        st2.append(Tl(None, [Buf()]))
                        P.dma("sp", l2v[tt][sub * 128:(sub + 1) * 128, blk * 512:(blk + 1) * 512], vs.ap, reads=[vs],
                              writes=[st2[-1]])
                RG = [[0, 1], [2, 3], [4, 5], [6, 7]]

                def issue_cc(tt=tt, st1=st1, st2=st2):
                    P.custom("pool", lambda e, i_=l1[tt], o_=g1[tt]: e.collective_compute("AllGather", ALU.bypass, replica_groups=RG,
                                                                                         ins=[i_[:, :]], outs=[o_[:, :]]),
                             reads=st1, writes=[g1dep[tt]], sem=ccsems[2 * tt], amount=1)
                    P.custom("pool", lambda e, i_=l2[tt], o_=g2[tt]: e.collective_compute("AllGather", ALU.bypass, replica_groups=RG,
                                                                                         ins=[i_[:, :]], outs=[o_[:, :]]),
                             reads=st2, writes=[g2dep[tt]], sem=ccsems[2 * tt + 1], amount=1)
                c.cc_pending.append(issue_cc)
                stage(12 if tt == 0 else 13)
                for k in range(NKC):
                    P.dma("sp", hT_d[k * 128:(k + 1) * 128, t0:t0 + TT], h[k].ap, reads=[h[k]], writes=[hdep[tt][k]])

        while getattr(c, "cc_pending", []):
            c.cc_pending.pop(0)()
        if doB:
            setup_mem(1)
            def load_krTs():
                for rk in range(2):
                    for t_ in range(NTILE):
                        c0_ = (rk * NTILE + t_) * TT
                        P.dma("sp", krTs.ap[0:64, c0_:c0_ + TT], g1[t_][rk * 1600 + 1536:rk * 1600 + 1600, :], reads=[g1dep[t_]],
                              writes=[krTs])
            for tt in range(NTILE):
                j = tt
                t0 = tt * TT
                for k in range(NKC):
                    P.dma("sp", h[k].ap, hT_d[k * 128:(k + 1) * 128, t0:t0 + TT], reads=[hdep[tt][k]], writes=[h[k]])
                rms_to(h, hn, G_NORM(1, 0), D, split=True)
                tables(t0)
                w = wload_std(b_w_in[0], 0, 16, 0, 512)
                sbk = STAT
                bks_ = mm4_kmajor(w, hn)
                for mi in range(4):
                    bk = bks_[mi]
                    act(cqf[mi], bk, AF.Copy)
                    stat_accum(sbk, cqf[mi], mi == 0, mi == 3)
                r = make_rstd(sbk, 512)
                for mi in range(4):
                    vstt(cqT[mi], cqf[mi], gcol(G_BQ + mi), r, ALU.mult, ALU.mult, reads=[gv])
                w = wload_std(b_w_in[0], 0, 16, 512, 512)
                for mi in range(4):
                    bk = nb()
                    for k in range(NKC):
                        mm(bk, w[:, k, mi * 128:(mi + 1) * 128], hn[k], start=(k == 0), stop=(k == NKC - 1))
                    evac_copy(qmT[mi], bk)
                mem_attention()
                uq = b_w_uq[0]
                for half in range(2):
                    cbase = half * 1152
                    v3 = lambda t, s_: t[:, s_, 0:4 * 1536].rearrange("p (k n) -> p k n", n=1536)
                    parts = [
                        (lambda t, s_: v3(t, s_)[:, :, 0:1152], uq[:, cbase:cbase + 1152].rearrange("(k p) n -> p k n", p=128)),
                    ]
                    for kq in range(4):
                        rsrc = uq[kq * 128:(kq + 1) * 128, cbase:cbase + 1152].rearrange("p (hh d) -> p hh d", d=192)
                        parts.append((lambda t, s_, kq=kq: v3(t, s_)[:, kq, 1152:1536].rearrange("p (hh d) -> p hh d", d=64)[:, :, 0:32],
                                      rsrc[:, :, 160:192]))
                        parts.append((lambda t, s_, kq=kq: v3(t, s_)[:, kq, 1152:1536].rearrange("p (hh d) -> p hh d", d=64)[:, :, 32:64],
                                      rsrc[:, :, 128:160]))
                    s_, slot = wload(parts)
                    w = Tl(wview(s_, 4, 1536), slot.bufs)
                    for hl in range(6):
                        hh = half * 6 + hl
                        bk = nb()
                        for k in range(4):
                            mm(bk, w[:, k, hl * 192:hl * 192 + 128], cqT[k], start=(k == 0), stop=(k == 3))
                        evac_copy(qnT[hh], bk)
                        ba = nb()
                        bb = nb()
                        for k in range(4):
                            mm(ba[0:64, :], w[:, k, hl * 192 + 128:hl * 192 + 192], cqT[k], start=(k == 0), stop=(k == 3))
                        for k in range(4):
                            mm(bb[0:64, :], w[:, k, 1152 + hl * 64:1152 + (hl + 1) * 64], cqT[k], start=(k == 0), stop=(k == 3))
                        rope_combine(qrT[hh], ba, bb)
                if tt == 0:
                    load_krTs()
                nblk_half = 4 * j + 4
                c.ring = [0, 1, 2]
                c.bank = 0
                for hh in range(12):
                    ob = banks[3 + 2 * (hh % 2)]
                    sbk = banks[4 + 2 * (hh % 2)]
                    first = True
                    chunks = []
                    for rk in range(2):
                        b0 = 0
                        while b0 < nblk_half:
                            nbk = min(4, nblk_half - b0)
                            chunks.append((rk, b0, nbk))
                            b0 += nbk
                    nblocks_total = 2 * nblk_half
                    pend = []
                    st_ = {"first": True, "done": 0}

                    def pv_stage(item, ob=ob, sbk=sbk, st_=st_, nblocks_total=nblocks_total):
                        vc_, bi, p, c0 = item
                        st_["done"] += 1
                        last = (st_["done"] == nblocks_total)
                        mm(ob[:, c0:TT], vc_[:, bi, :], p[:, c0:TT], start=st_["first"], stop=last, signal=True)
                        mm(sbk[:, c0:TT], ones, p[:, c0:TT], start=st_["first"], stop=last, signal=True)
                        st_["first"] = False

                    for (rk, b0, nbk) in chunks:
                        i_ = c.kvn
                        c.kvn = (c.kvn + 1) % NKV
                        kc_, vc_ = Kc[i_], Vc[i_]
                        key0 = rk * TOKC + b0 * 128
                        t_ = b0 // 4
                        P.dma("sp", kc_.ap[:, 0:TT], g1[t_][rk * 1600 + hh * 128:rk * 1600 + (hh + 1) * 128, :], reads=[g1dep[t_]],
                              writes=[kc_])
                        P.dma("sp", vc_.ap[:, 0:nbk, :],
                              g2v[t_][rk][:, hh * 128:(hh + 1) * 128].rearrange("(b p) d -> p b d", p=128),
                              reads=[g2dep[t_]], writes=[vc_])
                        for bi in range(nbk):
                            i = b0 + bi
                            if i < 4 * j:
                                c0 = 0
                                lp = None
                            else:
                                lp = i - 4 * j
                                c0 = lp * 128
                            kg0 = key0 + bi * 128
                            bk = nb()
                            mm(bk[:, c0:TT], kc_[:, bi * 128:(bi + 1) * 128], qnT[hh][:, c0:TT], start=True, stop=False, signal=False)
                            mm(bk[:, c0:TT], krTs[0:64, kg0:kg0 + 128], qrT[hh][0:64, c0:TT], start=False, stop=True)
                            p = ring(pT, "pTn")
                            act(p[:, c0:TT], bk[:, c0:TT], AF.Exp, scale=MLA_SCALE)
                            if lp is not None:
                                vtt(p[:, c0:c0 + 128], p[:, c0:c0 + 128], dmask[:, rk * 128:(rk + 1) * 128], ALU.mult)
                            pend.append((vc_, bi, p, c0))
                            if len(pend) > 2:
                                pv_stage(pend.pop(0))
                    while pend:
                        pv_stage(pend.pop(0))
                    rc = ring(tmpf, "tmpn")
                    vrecip(rc, sbk)
                    vtt(tokT[hh], ob, rc, ALU.mult)
                c.ring = [0, 1, 2, 3, 4, 5, 6]
                c.bank = 0
                out_proj_and_residual(b_w_out[0], tokT + memoT, G_NORM(1, 1))
                ffn(1)
                for sub in range(4):
                    for kq in range(4):
                        bk = nb()
                        for kk in range(4):
                            k = kq * 4 + kk
                            tr(bk[:, kk * 128:(kk + 1) * 128], h[k][:, sub * 128:(sub + 1) * 128], signal=(kk == 3))
                        evac_copy(xs[sub][:, kq * 512:(kq + 1) * 512], bk)
                    P.dma("sp", out[t0 + sub * 128:t0 + (sub + 1) * 128, :], xs[sub].ap, reads=[xs[sub]])


    except StopBuild:
        if mode == "A":
            def dump(tile, rb, cb, np_=128, ncol=512, conv=True):
                if conv:
                    t_ = ring(tmpf, "tmpn")
                    vcopy(t_[0:np_, 0:ncol], tile)
                    src = t_[0:np_, 0:ncol]
                else:
                    src = tile
                P.dma("sp", hT_d[rb * 128:rb * 128 + np_, cb * 512:cb * 512 + ncol], src.ap, reads=[src])
            sp_ = stop
            for k in range(NKC):
                if sp_ >= 4:
                    dump(h[k], k, 0, conv=False)
                if 5 <= sp_ < 10:
                    dump(hn[k], k, 1)
            for k in range(12):
                if 6 <= sp_ < 10:
                    dump(tokT[k], k, 2)
            for k in range(4):
                if 9 <= sp_ < 10:
                    dump(memoT[k], 12 + k, 2)
            dump(gv, 0, 3, ncol=256, conv=False)
            if sp_ >= 12:
                dump(Ctab, 5, 3, np_=64, conv=False)
                dump(Stab, 6, 3, np_=64, conv=False)
    fw = {"sp": [(P.sp_sems[i], P.sp_cnt[i]) for i in range(len(P.sp_sems)) if P.sp_cnt[i] > 0]}
    P.emit(fw)
    es.close()
    return nc, P


def _host_consts(r):
    ident = np.eye(128, dtype=np.float32)
    triu = np.triu(np.ones((128, 128), np.float32))
    ones_ = np.ones((128, 128), np.float32)
    zeros_ = np.zeros((128, 128), np.float32)
    def mk(rk):
        if rk < r:
            return ones_
        if rk == r:
            return triu
        return zeros_
    return np.concatenate([ident, triu, mk(0), mk(1)], axis=1)


def _shard_tokens(a, b, r):
    s = a.reshape(32, 128, *a.shape[1:])
    return np.ascontiguousarray(s[r::2].reshape(TOKC, *a.shape[1:]))


_NC_CACHE = {}


def _get_nc(mode):
    if mode not in _NC_CACHE:
        _NC_CACHE[mode] = build(mode)[0]
    return _NC_CACHE[mode]


def kernel(x, mem, positions, norm_gains, mem_norm, w_mem_kv, ffn_w_gate, ffn_w_up, ffn_w_down,
           a_w_in, a_ln_g, a_ln_b, a_w_s, a_b_s, a_w_out, kv_src_norm, w_kv_a, kv_norm, w_uk, w_uv,
           b_w_in, b_q_norm, b_w_uq, b_w_out):
    f = lambda a: np.ascontiguousarray(np.asarray(a, dtype=np.float32))
    x = f(x); mem = f(mem)
    positions = np.ascontiguousarray(np.asarray(positions, dtype=np.int32))
    vecs = np.zeros((256, 128), np.float32)
    vecs[0:128] = f(norm_gains).reshape(128, 128)
    vecs[128:144] = f(kv_src_norm).reshape(16, 128)
    vecs[144:176] = f(mem_norm).reshape(32, 128)
    vecs[176:180] = f(b_q_norm).reshape(4, 128)
    vecs[180:184] = f(kv_norm).reshape(4, 128)
    vecs[184:196] = f(a_ln_b).reshape(12, 128)
    vecs[196:208] = f(a_ln_g).reshape(12, 128)
    inv = (10000.0 ** (-np.arange(0, 64, 2, dtype=np.float32) / 64)).astype(np.float32)
    invf = np.concatenate([inv, inv]).reshape(64, 1).astype(np.float32)
    common = dict(vecs=vecs, invf=invf)
    w_mem_kv = f(w_mem_kv); ffn_w_gate = f(ffn_w_gate); ffn_w_up = f(ffn_w_up); ffn_w_down = f(ffn_w_down)
    lw = lambda l: {"w_mem_kv%d" % l: w_mem_kv[l], "ffn_w_gate%d" % l: ffn_w_gate[l], "ffn_w_up%d" % l: ffn_w_up[l],
                    "ffn_w_down%d" % l: ffn_w_down[l]}
    wA = dict(**lw(0), a_w_in=f(a_w_in), a_w_s=f(a_w_s), a_b_s=f(a_b_s), a_w_out=f(a_w_out), w_kv_a=f(w_kv_a), w_uk=f(w_uk), w_uv=f(w_uv))
    wB = dict(**lw(1), b_w_in=f(b_w_in), b_w_uq=f(b_w_uq), b_w_out=f(b_w_out))
    cores = list(range(8))
    percore = []
    for cid in cores:
        b, r = cid // 2, cid % 2
        percore.append(dict(cst=_host_consts(r), pos=_shard_tokens(positions[b], b, r).reshape(1, TOKC),
                            mem=np.ascontiguousarray(mem[b])))
    ncAB = _get_nc("AB")
    mapsAB = []
    for cid in cores:
        b, r = cid // 2, cid % 2
        m = dict(common); m.update(wA); m.update(wB); m.update(percore[cid])
        m["x"] = _shard_tokens(x[b], b, r)
        mapsAB.append(m)
    resB = run_bass_kernel_spmd(ncAB, mapsAB, core_ids=cores).results
    outp = np.empty((4, 32, 128, D), np.float32)
    for cid in cores:
        b, r = cid // 2, cid % 2
        outp[b, r::2] = resB[cid]["out"].reshape(16, 128, D)
    return outp.reshape(4, SEQ, D)
```
